# Optimizing a Trainium2 kernel written in Bass

```python
import math
import jax, jax.numpy as jnp
from jax import lax
import numpy as np

D_MODEL = 2048
BATCH = 16
SEQ = 2048
DEPTH = 4

N_MIXERS = 2
MEM_LEN = 256
MIX_WIDTH = D_MODEL
MEM_HEADS = 4
MEM_HEAD_DIM = MIX_WIDTH // (4 * MEM_HEADS)
MEM_WIDTH = MEM_HEADS * MEM_HEAD_DIM
SEQ_WIDTH = MIX_WIDTH - MEM_WIDTH

RWKV_HEAD = 64
RWKV_HEADS = SEQ_WIDTH // RWKV_HEAD
LORA_W = 96
LORA_A = 96
LORA_V = 64
LORA_G = 256
RWKV_GN_EPS = 1e-5 * RWKV_HEAD

RET_HEADS = 6
RET_DV = SEQ_WIDTH // RET_HEADS
RET_DK = RET_DV // 2
RET_CHUNK = 128
ROPE_BASE = 10000.0

D_FF = 5632
N_EXPERTS = 8
TOP_K = 2
D_FF_EXPERT = 5632

ALPHA = (2.0 * DEPTH) ** 0.25
BETA = (8.0 * DEPTH) ** -0.25
LN_EPS = 1e-5

kernel_name = "hybrid_rwkv7_retention_memattn_moe_deepnorm"


def layer_norm(x, g, b, eps=LN_EPS):
    xf = x.astype(jnp.float32)
    mu = jnp.mean(xf, -1, keepdims=True)
    var = jnp.mean(jnp.square(xf - mu), -1, keepdims=True)
    return ((xf - mu) * lax.rsqrt(var + eps)).astype(x.dtype) * g + b


def head_norm(y, g, b, eps):
    B, S, H, d = y.shape
    yf = y.astype(jnp.float32)
    mu = jnp.mean(yf, -1, keepdims=True)
    var = jnp.mean(jnp.square(yf - mu), -1, keepdims=True)
    yn = ((yf - mu) * lax.rsqrt(var + eps)).astype(y.dtype)
    return yn.reshape(B, S, H * d) * g + b


def split_cols(p, sizes):
    idx = [int(i) for i in np.cumsum(sizes)[:-1]]
    return jnp.split(p, idx, axis=-1)


def token_shift(p):
    return jnp.pad(p, ((0, 0), (1, 0), (0, 0)))[:, :-1]


def rwkv7_scan(r, w, k, v, kk, a):
    f32 = jnp.float32
    B, S, H, N = r.shape
    xs = tuple(jnp.moveaxis(t.astype(f32), 1, 0) for t in (r, w, k, v, kk, a))

    def step(state, inp):
        r_t, w_t, k_t, v_t, kk_t, a_t = inp
        sa = jnp.einsum('bhij,bhj->bhi', state, -kk_t)
        state = (state * w_t[:, :, None, :]
                 + sa[..., None] * (kk_t * a_t)[:, :, None, :]
                 + v_t[..., None] * k_t[:, :, None, :])
        y_t = jnp.einsum('bhij,bhj->bhi', state, r_t)
        return state, y_t

    s0 = jnp.zeros((B, H, N, N), f32)
    _, y = lax.scan(step, s0, xs)
    return jnp.moveaxis(y, 0, 1).astype(r.dtype)


def rwkv7_mixer(p, mu, up_w, up_a, up_g, up_v, vecs, v_first):
    B, S, _ = p.shape
    p = p + mu * (token_shift(p) - p)
    if up_v is None:
        r, k, v, xw, xa, xg = split_cols(p, [SEQ_WIDTH] * 3 + [LORA_W, LORA_A, LORA_G])
        w0, a0, k_k, k_a, r_k, gn_g, gn_b = vecs
        v_first = v
    else:
        r, k, v, xw, xa, xv, xg = split_cols(p, [SEQ_WIDTH] * 3 + [LORA_W, LORA_A, LORA_V, LORA_G])
        w0, a0, v0, k_k, k_a, r_k, gn_g, gn_b = vecs
        v = v + (v_first - v) * jax.nn.sigmoid(v0 + xv @ up_v)
    w_log = -jax.nn.softplus(-(w0 + jnp.tanh(xw) @ up_w)) - 0.5
    decay = jnp.exp(-jnp.exp(w_log.astype(jnp.float32))).astype(p.dtype)
    a = jax.nn.sigmoid(a0 + xa @ up_a)
    g = jax.nn.sigmoid(xg) @ up_g

    heads = lambda t: t.reshape(B, S, RWKV_HEADS, RWKV_HEAD)
    kk = heads(k * k_k).astype(jnp.float32)
    kk = (kk / jnp.maximum(jnp.sqrt(jnp.sum(jnp.square(kk), -1, keepdims=True)), 1e-12)).astype(p.dtype)
    k = k * (1.0 + (a - 1.0) * k_a)
    r_h, k_h, v_h = heads(r), heads(k), heads(v)
    y = rwkv7_scan(r_h, heads(decay), k_h, v_h, kk, heads(a))
    bonus = jnp.sum(r_h * k_h * r_k.reshape(RWKV_HEADS, RWKV_HEAD), -1, keepdims=True) * v_h
    y = head_norm(y, gn_g, gn_b, RWKV_GN_EPS) + bonus.reshape(B, S, SEQ_WIDTH)
    return y * g, v_first


def rotary(x, pos):
    d = x.shape[-1]
    inv = 1.0 / (ROPE_BASE ** (jnp.arange(0, d, 2, dtype=jnp.float32) / d))
    ang = pos.astype(jnp.float32)[..., None] * inv
    cos = jnp.cos(ang)[:, :, None, :]
    sin = jnp.sin(ang)[:, :, None, :]
    x1, x2 = jnp.split(x.astype(jnp.float32), 2, axis=-1)
    return jnp.concatenate([x1 * cos - x2 * sin, x1 * sin + x2 * cos], -1).astype(x.dtype)


def retention(q, k, v, pos):
    f32 = jnp.float32
    B, S, H, dk = q.shape
    dv = v.shape[-1]
    C = RET_CHUNK
    nC = S // C
    q = rotary(q, pos)
    k = rotary(k, pos) * (dk ** -0.5)
    log_g = jnp.log1p(-jnp.exp2(-5.0 - jnp.arange(H, dtype=f32)))
    idx = jnp.arange(C, dtype=f32)
    diff = idx[:, None] - idx[None, :]
    intra_decay = jnp.where(diff >= 0, jnp.exp(jnp.maximum(diff, 0.0)[None] * log_g[:, None, None]), 0.0)
    q_decay = jnp.exp((idx + 1.0)[:, None] * log_g[None, :])[None, :, :, None]
    k_decay = jnp.exp((C - 1.0 - idx)[:, None] * log_g[None, :])[None, :, :, None]
    chunk_decay = jnp.exp(C * log_g)[None, :, None, None]

    def chunks(t):
        return jnp.moveaxis(t.astype(f32).reshape(B, nC, C, H, t.shape[-1]), 1, 0)

    def step(R, inp):
        qc, kc, vc = inp
        s = jnp.einsum('bnhd,bmhd->bhnm', qc, kc) * intra_decay[None]
        o = jnp.einsum('bhnm,bmhe->bnhe', s, vc)
        o = o + jnp.einsum('bnhd,bhde->bnhe', qc * q_decay, R)
        R = R * chunk_decay + jnp.einsum('bmhd,bmhe->bhde', kc * k_decay, vc)
        return R, o

    R0 = jnp.zeros((B, H, dk, dv), f32)
    _, o = lax.scan(step, R0, (chunks(q), chunks(k), chunks(v)))
    return jnp.moveaxis(o, 0, 1).reshape(B, S, H, dv).astype(v.dtype)


def retention_mixer(p, positions, gn):
    B, S, _ = p.shape
    q, k, v, g = split_cols(p, [RET_HEADS * RET_DK, RET_HEADS * RET_DK, SEQ_WIDTH, SEQ_WIDTH])
    o = retention(q.reshape(B, S, RET_HEADS, RET_DK), k.reshape(B, S, RET_HEADS, RET_DK),
                  v.reshape(B, S, RET_HEADS, RET_DV), positions)
    return jax.nn.silu(g) * head_norm(o, gn[0], gn[1], LN_EPS)


def memory_attention(q, mem_k, mem_v):
    B, S, _ = q.shape
    q = q.reshape(B, S, MEM_HEADS, MEM_HEAD_DIM)
    s = jnp.einsum('bshd,bmhd->bhsm', q, mem_k).astype(jnp.float32) * (MEM_HEAD_DIM ** -0.5)
    p = jax.nn.softmax(s, axis=-1).astype(mem_v.dtype)
    return jnp.einsum('bhsm,bmhd->bshd', p, mem_v).reshape(B, S, MEM_WIDTH)


def swiglu(x, w_gate, w_up, w_down):
    return (jax.nn.silu(x @ w_gate) * (x @ w_up)) @ w_down


def moe_swiglu(x, w_router, w_gate, w_up, w_down):
    logits = (x @ w_router).astype(jnp.float32)
    top_val, top_idx = lax.top_k(logits, TOP_K)
    top_w = jax.nn.softmax(top_val, axis=-1)
    combine = jnp.sum(jax.nn.one_hot(top_idx, N_EXPERTS, dtype=jnp.float32) * top_w[..., None], axis=-2)
    combine = combine.astype(x.dtype)
    y = jnp.zeros_like(x)
    for e in range(N_EXPERTS):
        y = y + combine[..., e:e + 1] * swiglu(x, w_gate[e], w_up[e], w_down[e])
    return y


def setup_inputs(seed: int = 0) -> dict:
    key = jax.random.key(seed)
    keys = iter(jax.random.split(key, 128))
    f32 = jnp.float32

    def normal(shape, scale):
        return jax.random.normal(next(keys), shape, f32) * scale

    def uniform(shape, lo, hi):
        return jax.random.uniform(next(keys), shape, f32, lo, hi)

    def gain(n):
        return 1.0 + normal((n,), 0.02)

    def proj_in(sizes, scales):
        col_scale = jnp.concatenate([jnp.full((n,), s, f32) for n, s in zip(sizes, scales)])
        return normal((D_MODEL, sum(sizes)), D_MODEL ** -0.5) * col_scale

    def norms():
        return jnp.stack([gain(D_MODEL), normal((D_MODEL,), 0.02), gain(D_MODEL), normal((D_MODEL,), 0.02)])

    inp = {}
    inp["x"] = normal((BATCH, SEQ, D_MODEL), 1.0)
    inp["mem"] = normal((BATCH, MEM_LEN, D_MODEL), 1.0)
    offset = jax.random.randint(next(keys), (BATCH, 1), 0, 4096, dtype=jnp.int32)
    inp["positions"] = offset + jnp.arange(SEQ, dtype=jnp.int32)[None, :]
    inp["mem_w_kv"] = proj_in([MEM_WIDTH, MEM_WIDTH], [1.0, BETA])

    def add_rwkv(pre, with_v):
        sizes = [SEQ_WIDTH] * 3 + [LORA_W, LORA_A] + ([LORA_V] if with_v else []) + [LORA_G]
        scales = [1.0, 1.0, BETA, 1.0, 1.0] + ([1.0] if with_v else []) + [1.0]
        inp[pre + "w_in"] = proj_in(sizes + [MEM_WIDTH], scales + [1.0])
        inp[pre + "mu"] = uniform((sum(sizes),), 0.0, 1.0)
        inp[pre + "lora_w"] = normal((LORA_W, SEQ_WIDTH), 0.5 * LORA_W ** -0.5)
        inp[pre + "lora_a"] = normal((LORA_A, SEQ_WIDTH), 0.5 * LORA_A ** -0.5)
        if with_v:
            inp[pre + "lora_v"] = normal((LORA_V, SEQ_WIDTH), 0.5 * LORA_V ** -0.5)
        inp[pre + "lora_g"] = normal((LORA_G, SEQ_WIDTH), LORA_G ** -0.5)
        vecs = [uniform((SEQ_WIDTH,), -6.5, -1.5), normal((SEQ_WIDTH,), 0.5)]
        if with_v:
            vecs.append(normal((SEQ_WIDTH,), 0.5))
        vecs += [0.85 + normal((SEQ_WIDTH,), 0.1), 1.0 + normal((SEQ_WIDTH,), 0.1),
                 normal((SEQ_WIDTH,), 0.1), gain(SEQ_WIDTH), normal((SEQ_WIDTH,), 0.02)]
        inp[pre + "vecs"] = jnp.stack(vecs)
        inp[pre + "w_out"] = normal((MIX_WIDTH, D_MODEL), BETA * MIX_WIDTH ** -0.5)
        inp[pre + "norms"] = norms()
        inp[pre + "ffn_gate"] = normal((D_MODEL, D_FF), BETA * D_MODEL ** -0.5)
        inp[pre + "ffn_up"] = normal((D_MODEL, D_FF), BETA * D_MODEL ** -0.5)
        inp[pre + "ffn_down"] = normal((D_FF, D_MODEL), BETA * D_FF ** -0.5)

    def add_ret(pre):
        qk = RET_HEADS * RET_DK
        inp[pre + "w_in"] = proj_in([qk, qk, SEQ_WIDTH, SEQ_WIDTH, MEM_WIDTH], [1.0, 1.0, BETA, 1.0, 1.0])
        inp[pre + "gn"] = jnp.stack([gain(SEQ_WIDTH), normal((SEQ_WIDTH,), 0.02)])
        inp[pre + "w_out"] = normal((MIX_WIDTH, D_MODEL), BETA * MIX_WIDTH ** -0.5)
        inp[pre + "norms"] = norms()
        inp[pre + "router"] = normal((D_MODEL, N_EXPERTS), D_MODEL ** -0.5)
        inp[pre + "moe_gate"] = normal((N_EXPERTS, D_MODEL, D_FF_EXPERT), BETA * D_MODEL ** -0.5)
        inp[pre + "moe_up"] = normal((N_EXPERTS, D_MODEL, D_FF_EXPERT), BETA * D_MODEL ** -0.5)
        inp[pre + "moe_down"] = normal((N_EXPERTS, D_FF_EXPERT, D_MODEL), BETA * D_FF_EXPERT ** -0.5)

    add_rwkv("l0_", False)
    add_ret("l1_")
    add_rwkv("l2_", True)
    add_ret("l3_")
    return inp


def reference(x, mem, positions, mem_w_kv,
              l0_w_in, l0_mu, l0_lora_w, l0_lora_a, l0_lora_g, l0_vecs, l0_w_out, l0_norms,
              l0_ffn_gate, l0_ffn_up, l0_ffn_down,
              l1_w_in, l1_gn, l1_w_out, l1_norms, l1_router, l1_moe_gate, l1_moe_up, l1_moe_down,
              l2_w_in, l2_mu, l2_lora_w, l2_lora_a, l2_lora_v, l2_lora_g, l2_vecs, l2_w_out, l2_norms,
              l2_ffn_gate, l2_ffn_up, l2_ffn_down,
              l3_w_in, l3_gn, l3_w_out, l3_norms, l3_router, l3_moe_gate, l3_moe_up, l3_moe_down):
    B, M, _ = mem.shape
    mem_kv = mem @ mem_w_kv
    mem_k, mem_v = [t.reshape(B, M, MEM_HEADS, MEM_HEAD_DIM) for t in jnp.split(mem_kv, 2, axis=-1)]

    layers = [
        dict(w_in=l0_w_in, mixer=(l0_mu, l0_lora_w, l0_lora_a, l0_lora_g, None, l0_vecs),
             w_out=l0_w_out, norms=l0_norms, ffn=(l0_ffn_gate, l0_ffn_up, l0_ffn_down)),
        dict(w_in=l1_w_in, mixer=(l1_gn,), w_out=l1_w_out, norms=l1_norms,
             ffn=(l1_router, l1_moe_gate, l1_moe_up, l1_moe_down)),
        dict(w_in=l2_w_in, mixer=(l2_mu, l2_lora_w, l2_lora_a, l2_lora_g, l2_lora_v, l2_vecs),
             w_out=l2_w_out, norms=l2_norms, ffn=(l2_ffn_gate, l2_ffn_up, l2_ffn_down)),
        dict(w_in=l3_w_in, mixer=(l3_gn,), w_out=l3_w_out, norms=l3_norms,
             ffn=(l3_router, l3_moe_gate, l3_moe_up, l3_moe_down)),
    ]

    v_first = None
    for i in range(DEPTH):
        L = layers[i]
        p = x @ L["w_in"]
        p_seq, q_mem = p[..., :-MEM_WIDTH], p[..., -MEM_WIDTH:]
        if i % N_MIXERS == 0:
            y_seq, v_first = rwkv7_mixer(p_seq, *L["mixer"], v_first)
        else:
            y_seq = retention_mixer(p_seq, positions, *L["mixer"])
        y_mem = memory_attention(q_mem, mem_k, mem_v)
        h = jnp.concatenate([y_seq, y_mem], axis=-1) @ L["w_out"]
        nrm = L["norms"]
        x = layer_norm(ALPHA * x + h, nrm[0], nrm[1])
        if i % 2 == 0:
            f = swiglu(x, *L["ffn"])
        else:
            f = moe_swiglu(x, *L["ffn"])
        x = layer_norm(ALPHA * x + f, nrm[2], nrm[3])
    return x
```

```python
import numpy as np
from contextlib import ExitStack
import concourse.bass as bass
import concourse.mybir as mybir
from concourse.bass_utils import run_bass_kernel_spmd

F32 = mybir.dt.float32
BF16 = mybir.dt.bfloat16
I32 = mybir.dt.int32
AF = mybir.ActivationFunctionType
ALU = mybir.AluOpType
AX = mybir.AxisListType

D = 2048
MEMLEN = 256
SEQW = 1536
MEMW = 512
NH = 24
DFF = 5632
NE = 8
ALPHA = float(8.0 ** 0.25)
LN_EPS = 1e-5
GN_EPS_RWKV = 1e-5 * 64
RET_H = 6
RET_DK = 128
RET_DV = 256
TWO_PI = float(2 * np.pi)


EMBED = True


class Buf:
    __slots__ = ("name", "lw", "rd", "dsem", "persist")

    def __init__(self, name, persist=False):
        self.name = name
        self.lw = None
        self.rd = []
        self.dsem = None
        self.persist = persist


class KB:
    def __init__(self, nc):
        self.nc = nc
        self.es = ExitStack()
        self.eng = {"pe": nc.tensor, "act": nc.scalar, "dve": nc.vector, "pool": nc.gpsimd, "sp": nc.sync}
        self.sems = {}
        self.cnt = {}
        self.waited = {e: {} for e in self.eng}
        for e in self.eng:
            self._newsem("E" + e)
        self.free_dsems = []
        self.ndsem = 0
        self.bufs = []
        self.uid = 0
        self.ninstr = 0

    def _newsem(self, name):
        self.sems[name] = self.es.enter_context(self.nc.semaphore(name))
        self.cnt[name] = 0

    def get_dsem(self):
        if self.free_dsems:
            return self.free_dsems.pop()
        name = "D%d" % self.ndsem
        self.ndsem += 1
        self._newsem(name)
        return name

    def buf(self, name, persist=False):
        b = Buf(name, persist)
        self.bufs.append(b)
        return b

    def name(self, base):
        self.uid += 1
        return "%s_%d" % (base, self.uid)

    def _wait(self, e, toks, embed=False):
        best = {}
        for t in toks:
            if t is None:
                continue
            s, v = t
            if s[0] == "D":
                v = self.cnt[s]
            if best.get(s, 0) < v:
                best[s] = v
        need = []
        for s, v in best.items():
            if s == "E" + e and e in ("pe", "sp"):
                continue
            if self.waited[e].get(s, 0) >= v:
                continue
            need.append((s, v))
            self.waited[e][s] = v
        last = None
        if embed and need:
            last = need.pop()
        for s, v in need:
            self.eng[e].wait_ge(self.sems[s], v)
            self.ninstr += 1
        return last

    def _deps(self, R, W, group):
        toks = [b.lw for b in R]
        for b in W:
            if not (group and b.lw is not None and b.lw[0][0] == "D"):
                toks.append(b.lw)
            toks.extend(b.rd)
        return toks

    def _update(self, tok, R, W):
        for b in R:
            b.rd = [t for t in b.rd if t[0] != tok[0]] + [tok]
        for b in W:
            b.lw = tok
            b.rd = []

    def op(self, e, fn, R=(), W=()):
        last = self._wait(e, self._deps(R, W, False), EMBED)
        ins = fn()
        if last is not None:
            ins._wait_ge(self.sems[last[0]], last[1])
        s = "E" + e
        self.cnt[s] += 1
        ins.then_inc(self.sems[s], 1)
        self.ninstr += 1
        tok = (s, self.cnt[s])
        self._update(tok, R, W)
        return tok

    def dma(self, q, out, in_, R=(), W=(), group=True, sbuf=None):
        last = self._wait(q, self._deps(R, W, group), EMBED)
        ins = self.eng[q].dma_start(out=out, in_=in_)
        if last is not None:
            ins._wait_ge(self.sems[last[0]], last[1])
        if sbuf.dsem is None:
            sbuf.dsem = self.get_dsem()
        s = sbuf.dsem
        self.cnt[s] += 16
        ins.then_inc(self.sems[s], 16)
        self.ninstr += 1
        tok = (s, self.cnt[s])
        self._update(tok, R, W)
        return tok

    def barrier(self):
        for e in self.eng:
            for s in self.sems:
                v = self.cnt[s]
                if v > self.waited[e].get(s, 0):
                    self.eng[e].wait_ge(self.sems[s], v)
                    self.ninstr += 1
                    self.waited[e][s] = v
        keep = []
        for b in self.bufs:
            b.lw = None
            b.rd = []
            if b.dsem is not None:
                self.free_dsems.append(b.dsem)
                b.dsem = None
            if b.persist:
                keep.append(b)
        self.bufs = keep


class Tile:
    def __init__(self, kb, st, name, shape, dtype, persist=False):
        self.t = st.enter_context(kb.nc.sbuf_tensor(kb.name(name), list(shape), dtype))
        self.b = kb.buf(name, persist)
        self.shape = shape

    def __getitem__(self, k):
        return self.t[k]


def colvec(v, n=None):
    v = np.asarray(v, np.float32).reshape(-1)
    nn = v.shape[0]
    nch = (nn + 127) // 128
    o = np.zeros((nch * 128,), np.float32)
    o[:nn] = v
    return np.ascontiguousarray(o.reshape(nch, 128).T)


class G:
    pass


def build(NB, S, layers, stop_after=None, dbg=None, ext_in=(), skip=()):
    T = NB * S
    nc = bass.Bass("TRN2", target_bir_lowering=False)
    kb = KB(nc)
    g = G()
    g.nc, g.kb, g.NB, g.S, g.T = nc, kb, NB, S, T
    g.din = {}

    def inp(name, shape, dt=F32):
        h = nc.dram_tensor(name, list(shape), dt, kind="ExternalInput")
        g.din[name] = h
        return h

    def scratch(name, shape, dt=F32):
        if name in ext_in:
            return nc.dram_tensor(name, list(shape), dt, kind="ExternalInput")
        return nc.dram_tensor(name, list(shape), dt)
    g.skip = skip

    inp("xT", [D, T])
    inp("memT", [D, NB * MEMLEN])
    inp("pos", [1, T], I32)
    inp("mem_w_kv", [D, 2 * MEMW])
    inp("c_ident", [128, 128])
    inp("c_ones", [128, 128])
    inp("c_blk", [128, 128])
    inp("c_swap", [128, 128])
    inp("c_ysel", [128, 32 * 64])
    inp("c_rot", [128, 4])
    inp("c_retT", [128, RET_H * 128])
    inp("c_retq", [128, RET_H * 128])
    inp("c_retk", [128, RET_H])
    inp("c_retc", [128, RET_H])
    LW = []
    for li in range(4):
        p = "l%d_" % li
        L = G()
        LW.append(L)
        if li not in layers:
            continue
        L.rwkv = (li % 2 == 0)
        _ne = 1 if L.rwkv else NE
        L.wtG = scratch(p + "wtG", [_ne * 44, 128, 16 * 128], BF16)
        L.wtU = scratch(p + "wtU", [_ne * 44, 128, 16 * 128], BF16)
        L.wtD = scratch(p + "wtD", [_ne * 16, 128, 44 * 128], BF16)
        L.withv = (li == 2)
        L.C = 5568 if li == 0 else (5632 if li == 2 else 5120)
        L.w_in = inp(p + "w_in", [D, L.C])
        L.w_out = inp(p + "w_out", [D, D])
        L.norms = inp(p + "norms", [128, 4 * 16])
        if L.rwkv:
            L.nmix = L.C - MEMW
            L.mu = inp(p + "mu", [128, 41])
            L.lora_w = inp(p + "lora_w", [96, SEQW])
            L.lora_a = inp(p + "lora_a", [96, SEQW])
            if L.withv:
                L.lora_v = inp(p + "lora_v", [64, SEQW])
            L.lora_g = inp(p + "lora_g", [256, SEQW])
            L.vecs = inp(p + "vecs", [128, (8 if L.withv else 7) * 12])
            L.ffn_gate = inp(p + "ffn_gate", [D, DFF])
            L.ffn_up = inp(p + "ffn_up", [D, DFF])
            L.ffn_down = inp(p + "ffn_down", [DFF, D])
        else:
            L.gn = inp(p + "gn", [128, 2 * 12])
            L.router = inp(p + "router", [D, NE])
            L.ffn_gate = inp(p + "moe_gate", [NE * D, DFF])
            L.ffn_up = inp(p + "moe_up", [NE * D, DFF])
            L.ffn_down = inp(p + "moe_down", [NE * DFF, D])
    g.out = nc.dram_tensor("outT", [D, T], F32, kind="ExternalOutput")

    g.xres = [scratch("xresA", [D, T]), scratch("xresB", [D, T])]
    g.Pd = scratch("Pd", [5632, T])
    g.Zd = scratch("Zd", [D, T])
    g.Ymix = scratch("Ymix", [D, T], BF16)
    g.Q = [scratch("Q%d" % i, [NB, SEQW, S]) for i in range(5)]
    for i in range(5):
        setattr(g, "Q%d" % i, g.Q[i])
    g.Vtok = scratch("Vtok", [NB, S, SEQW])
    g.Vfirst = scratch("Vfirst", [SEQW, T])
    g.Bonus = scratch("Bonus", [SEQW, T])
    g.Gd = scratch("Gd", [SEQW, T])
    g.Ytok = scratch("Ytok", [NB, S, SEQW])
    g.CombT = scratch("CombT", [NE, T])

    pst = kb.es
    g.ident = Tile(kb, pst, "ident", [128, 128], F32, True)
    g.identb = Tile(kb, pst, "identb", [128, 128], BF16, True)
    g.ones = Tile(kb, pst, "ones", [128, 128], F32, True)
    g.blk = Tile(kb, pst, "blk", [128, 128], F32, True)
    g.swap = Tile(kb, pst, "swap", [128, 128], F32, True)
    g.memk = Tile(kb, pst, "memk", [128, 4, NB, MEMLEN], BF16, True)
    g.memv = Tile(kb, pst, "memv", [128, NB, 2, MEMW], BF16, True)
    g.ps = pst.enter_context(nc.psum_tensor("psum_all", [128, 8, 512], F32))
    g.pb = [kb.buf("psb%d" % i, True) for i in range(8)]
    g.pbi = 0

    for name, t in (("c_ident", g.ident), ("c_ones", g.ones), ("c_blk", g.blk), ("c_swap", g.swap)):
        kb.dma("sp", t[:], g.din[name][:, :], W=[t.b], sbuf=t.b)
    kb.dma("pool", g.identb[:], g.din["c_ident"][:, :], W=[g.identb.b], sbuf=g.identb.b)
    kb.barrier()

    stage_setup_mem(g)
    if stop_after == "mem":
        return finish(g, dbg)
    cur = 0
    xin = g.din["xT"]
    for li in layers:
        L = LW[li]
        xo1 = g.xres[0]
        xo2 = g.xres[1]
        if "proj" not in g.skip:
            stage_proj(g, L, xin)
        if stop_after == "proj%d" % li:
            return finish(g, dbg)
        if "mix" in g.skip:
            pass
        elif L.rwkv:
            stage_rwkv_prep(g, L)
            if stop_after == "prep%d" % li:
                return finish(g, dbg)
            stage_rwkv_scan(g, L)
            if stop_after == "scan%d" % li:
                return finish(g, dbg)
            stage_rwkv_post(g, L)
        else:
            stage_retention(g, L)
        if stop_after == "mix%d" % li:
            return finish(g, dbg)
        if "memattn" not in g.skip:
            stage_memattn(g, L)
        if stop_after == "mem%d" % li:
            return finish(g, dbg)
        stage_wout(g, L, xin)
        stage_ln(g, L, 0, xo1)
        if stop_after == "ln1_%d" % li:
            return finish(g, dbg)
        if not L.rwkv:
            stage_router(g, L, xo1)
            if stop_after == "router%d" % li:
                return finish(g, dbg)
        stage_ffn(g, L, xo1)
        last = (li == layers[-1])
        stage_ln(g, L, 2, g.out if last else xo2)
        if stop_after == "ln2_%d" % li:
            return finish(g, dbg)
        xin = xo2
    return finish(g, dbg)


def finish(g, dbg):
    kb, nc = g.kb, g.nc
    kb.barrier()
    for name in (dbg or ()):
        src = getattr(g, name)
        o = nc.dram_tensor("dbg_" + name, list(src.shape), src.dtype, kind="ExternalOutput")
        db = kb.buf("dbg")
        if len(src.shape) == 2:
            kb.dma("sp", o[:, :], src[:, :], sbuf=db)
        else:
            kb.dma("sp", o[:, :, :], src[:, :, :], sbuf=db)
    kb.barrier()
    g.kb.es.close()
    return g


def psum_bank(g):
    i = g.pbi
    g.pbi = (g.pbi + 1) % 8
    return g.ps[:, i, :], g.pb[i]


def wtile_ap(W, r0, K, c0, ncol):
    return W[r0:r0 + K, c0:c0 + ncol].rearrange("(kc p) n -> p kc n", p=128)


def act_T_ap(A, r0, K, t0, nt):
    return A[r0:r0 + K, t0:t0 + nt].rearrange("(kc p) t -> p kc t", p=128)


def stage_setup_mem(g):
    kb, nc = g.kb, g.nc
    NB = g.NB
    with ExitStack() as st:
        memT = Tile(kb, st, "memT", [128, 16, NB * MEMLEN], BF16)
        wkv = Tile(kb, st, "wkv", [128, 16, 2 * MEMW], BF16)
        kb.dma("pool", memT[:], act_T_ap(g.din["memT"], 0, D, 0, NB * MEMLEN), W=[memT.b], sbuf=memT.b)
        kb.dma("pool", wkv[:], wtile_ap(g.din["mem_w_kv"], 0, D, 0, 2 * MEMW), W=[wkv.b], sbuf=wkv.b)
        for h in range(4):
            ps, pb = psum_bank(g)
            for kc in range(16):
                kb.op("pe", lambda kc=kc: nc.tensor.matmul(ps[:, 0:NB * MEMLEN], wkv[:, kc, h * 128:(h + 1) * 128],
                                                           memT[:, kc, :], start=(kc == 0), stop=(kc == 15)),
                      R=[wkv.b, memT.b], W=[pb])
            kb.op("act", lambda: nc.scalar.copy(out=g.memk[:, h, :, :].rearrange("p b m -> p (b m)"), in_=ps[:, 0:NB * MEMLEN]),
                  R=[pb], W=[g.memk.b])
        for b in range(NB):
            for mc in range(2):
                ps, pb = psum_bank(g)
                c0 = b * MEMLEN + mc * 128
                for kc in range(16):
                    kb.op("pe", lambda kc=kc: nc.tensor.matmul(ps[:, 0:MEMW], memT[:, kc, c0:c0 + 128],
                                                               wkv[:, kc, MEMW:2 * MEMW], start=(kc == 0), stop=(kc == 15)),
                          R=[wkv.b, memT.b], W=[pb])
                kb.op("act", lambda: nc.scalar.copy(out=g.memv[:, b, mc, :], in_=ps[:, 0:MEMW]), R=[pb], W=[g.memv.b])
        kb.barrier()


def stage_proj(g, L, xin):
    kb, nc, T = g.kb, g.nc, g.T
    TSB = min(T, 2048)
    with ExitStack() as st:
        xT = Tile(kb, st, "xT", [128, 16, TSB], BF16)
        wt = [Tile(kb, st, "wt", [128, 16, 256], BF16) for _ in range(2)]
        ot = [Tile(kb, st, "ot", [128, TSB], F32) for _ in range(2)]
        wi = 0
        oi = 0
        for t0 in range(0, T, TSB):
            kb.dma("pool", xT[:], act_T_ap(xin, 0, D, t0, TSB), W=[xT.b], sbuf=xT.b)
            for c0 in range(0, L.C, 256):
                ncol = min(256, L.C - c0)
                w = wt[wi]
                wi ^= 1
                kb.dma("pool", w[:, :, 0:ncol], wtile_ap(L.w_in, 0, D, c0, ncol), W=[w.b], sbuf=w.b)
                for cc in range(0, ncol, 128):
                    m = min(128, ncol - cc)
                    o = ot[oi]
                    oi ^= 1
                    for tb in range(0, TSB, 512):
                        ps, pb = psum_bank(g)
                        for kc in range(16):
                            kb.op("pe", lambda kc=kc: nc.tensor.matmul(ps[0:m, :], w[:, kc, cc:cc + m], xT[:, kc, tb:tb + 512],
                                                                       start=(kc == 0), stop=(kc == 15)),
                                  R=[w.b, xT.b], W=[pb])
                        kb.op("act", lambda: nc.scalar.copy(out=o[0:m, tb:tb + 512], in_=ps[0:m, :]), R=[pb], W=[o.b])
                    kb.dma("sp", g.Pd[c0 + cc:c0 + cc + m, t0:t0 + TSB], o[0:m, :], R=[o.b], sbuf=o.b)
        kb.barrier()


def stage_wout(g, L, xin):
    kb, nc, T = g.kb, g.nc, g.T
    TSB = min(T, 2048)
    with ExitStack() as st:
        yT = Tile(kb, st, "yT", [128, 16, TSB], BF16)
        wt = [Tile(kb, st, "wt", [128, 16, 256], BF16) for _ in range(2)]
        xt = [Tile(kb, st, "xt", [128, TSB], F32) for _ in range(2)]
        wi = 0
        oi = 0
        for t0 in range(0, T, TSB):
            kb.dma("sp", yT[:], act_T_ap(g.Ymix, 0, D, t0, TSB), W=[yT.b], sbuf=yT.b)
            for c0 in range(0, D, 256):
                w = wt[wi]
                wi ^= 1
                kb.dma("pool", w[:], wtile_ap(L.w_out, 0, D, c0, 256), W=[w.b], sbuf=w.b)
                for cc in range(0, 256, 128):
                    o = xt[oi]
                    oi ^= 1
                    kb.dma("sp", o[:], xin[c0 + cc:c0 + cc + 128, t0:t0 + TSB], W=[o.b], sbuf=o.b)
                    for tb in range(0, TSB, 512):
                        ps, pb = psum_bank(g)
                        for kc in range(16):
                            kb.op("pe", lambda kc=kc: nc.tensor.matmul(ps, w[:, kc, cc:cc + 128], yT[:, kc, tb:tb + 512],
                                                                       start=(kc == 0), stop=(kc == 15)),
                                  R=[w.b, yT.b], W=[pb])
                        kb.op("dve", lambda: nc.vector.scalar_tensor_tensor(out=o[:, tb:tb + 512], in0=o[:, tb:tb + 512], scalar=ALPHA,
                                                                            in1=ps, op0=ALU.mult, op1=ALU.add),
                              R=[pb, o.b], W=[o.b])
                    kb.dma("sp", g.Zd[c0 + cc:c0 + cc + 128, t0:t0 + TSB], o[:], R=[o.b], sbuf=o.b)
        kb.barrier()


def stage_ln(g, L, which, dst):
    kb, nc, T = g.kb, g.nc, g.T
    TB = 512
    with ExitStack() as st:
        nrm = Tile(kb, st, "nrm", [128, 64], F32)
        kb.dma("sp", nrm[:], L.norms[:, :], W=[nrm.b], sbuf=nrm.b)
        zt = [Tile(kb, st, "z", [128, 16, TB], F32) for _ in range(2)]
        sq = Tile(kb, st, "sq", [128, 16, TB], F32)
        mean = Tile(kb, st, "mean", [128, TB], F32)
        rstd = Tile(kb, st, "rstd", [128, TB], F32)
        zi = 0
        for t0 in range(0, T, TB):
            z = zt[zi]
            zi ^= 1
            kb.dma("sp", z[:], act_T_ap(g.Zd, 0, D, t0, TB), W=[z.b], sbuf=z.b)
            ps, pb = psum_bank(g)
            for kc in range(16):
                kb.op("pe", lambda kc=kc: nc.tensor.matmul(ps, g.ones[:], z[:, kc, :], start=(kc == 0), stop=(kc == 15)),
                      R=[g.ones.b, z.b], W=[pb])
            kb.op("act", lambda: nc.scalar.mul(out=mean[:], in_=ps, mul=1.0 / D), R=[pb], W=[mean.b])
            kb.op("dve", lambda: nc.vector.tensor_tensor(out=z[:], in0=z[:], in1=mean[:].unsqueeze(1).to_broadcast([128, 16, TB]),
                                                         op=ALU.subtract), R=[z.b, mean.b], W=[z.b])
            kb.op("act", lambda: nc.scalar.activation(out=sq[:], in_=z[:], func=AF.Square), R=[z.b], W=[sq.b])
            ps2, pb2 = psum_bank(g)
            for kc in range(16):
                kb.op("pe", lambda kc=kc: nc.tensor.matmul(ps2, g.ones[:], sq[:, kc, :], start=(kc == 0), stop=(kc == 15)),
                      R=[g.ones.b, sq.b], W=[pb2])
            kb.op("dve", lambda: nc.vector.tensor_scalar(out=rstd[:], in0=ps2, scalar1=1.0 / D, scalar2=LN_EPS,
                                                         op0=ALU.mult, op1=ALU.add), R=[pb2], W=[rstd.b])
            kb.op("act", lambda: nc.scalar.sqrt(out=rstd[:], in_=rstd[:]), R=[rstd.b], W=[rstd.b])
            kb.op("dve", lambda: nc.vector.reciprocal(out=rstd[:], in_=rstd[:]), R=[rstd.b], W=[rstd.b])
            kb.op("dve", lambda: nc.vector.tensor_tensor(out=z[:], in0=z[:], in1=rstd[:].unsqueeze(1).to_broadcast([128, 16, TB]),
                                                         op=ALU.mult), R=[z.b, rstd.b], W=[z.b])
            for kc in range(16):
                kb.op("act", lambda kc=kc: nc.scalar.activation(out=z[:, kc, :], in_=z[:, kc, :], func=AF.Identity,
                                                                scale=nrm[:, which * 16 + kc:which * 16 + kc + 1],
                                                                bias=nrm[:, (which + 1) * 16 + kc:(which + 1) * 16 + kc + 1]),
                      R=[z.b, nrm.b], W=[z.b])
            kb.dma("sp", act_T_ap(dst, 0, D, t0, TB), z[:], R=[z.b], sbuf=z.b)
        kb.barrier()


def stage_router(g, L, xsrc):
    kb, nc, T = g.kb, g.nc, g.T
    with ExitStack() as st:
        rw = Tile(kb, st, "rw", [128, 16, NE], F32)
        kb.dma("sp", rw[:], wtile_ap(L.router, 0, D, 0, NE), W=[rw.b], sbuf=rw.b)
        xt = [Tile(kb, st, "xr", [128, 16, 128], F32) for _ in range(2)]
        lg = Tile(kb, st, "lg", [128, NE], F32)
        m1 = Tile(kb, st, "m1", [128, 1], F32)
        m2 = Tile(kb, st, "m2", [128, 1], F32)
        k1 = Tile(kb, st, "k1", [128, NE], F32)
        k2 = Tile(kb, st, "k2", [128, NE], F32)
        l2 = Tile(kb, st, "l2", [128, NE], F32)
        w1 = Tile(kb, st, "w1", [128, 1], F32)
        w2 = Tile(kb, st, "w2", [128, 1], F32)
        cb = Tile(kb, st, "cb", [128, NE], F32)
        cT = [Tile(kb, st, "cT", [NE, 128], F32) for _ in range(2)]
        xi = 0
        for t0 in range(0, T, 128):
            x = xt[xi]
            c = cT[xi]
            xi ^= 1
            kb.dma("sp", x[:], act_T_ap(xsrc, 0, D, t0, 128), W=[x.b], sbuf=x.b)
            ps, pb = psum_bank(g)
            for kc in range(16):
                kb.op("pe", lambda kc=kc: nc.tensor.matmul(ps[:, 0:NE], x[:, kc, :], rw[:, kc, :], start=(kc == 0), stop=(kc == 15)),
                      R=[x.b, rw.b], W=[pb])
            kb.op("act", lambda: nc.scalar.copy(out=lg[:], in_=ps[:, 0:NE]), R=[pb], W=[lg.b])
            kb.op("dve", lambda: nc.vector.reduce_max(out=m1[:], in_=lg[:], axis=AX.X), R=[lg.b], W=[m1.b])
            kb.op("dve", lambda: nc.vector.tensor_scalar(out=k1[:], in0=lg[:], scalar1=m1[:, 0:1], scalar2=None, op0=ALU.is_equal),
                  R=[lg.b, m1.b], W=[k1.b])
            kb.op("dve", lambda: nc.vector.scalar_tensor_tensor(out=l2[:], in0=k1[:], scalar=-1e30, in1=lg[:], op0=ALU.mult, op1=ALU.add),
                  R=[k1.b, lg.b], W=[l2.b])
            kb.op("dve", lambda: nc.vector.reduce_max(out=m2[:], in_=l2[:], axis=AX.X), R=[l2.b], W=[m2.b])
            kb.op("dve", lambda: nc.vector.tensor_scalar(out=k2[:], in0=l2[:], scalar1=m2[:, 0:1], scalar2=None, op0=ALU.is_equal),
                  R=[l2.b, m2.b], W=[k2.b])
            kb.op("dve", lambda: nc.vector.tensor_tensor(out=w2[:], in0=m2[:], in1=m1[:], op=ALU.subtract), R=[m1.b, m2.b], W=[w2.b])
            kb.op("act", lambda: nc.scalar.activation(out=w2[:], in_=w2[:], func=AF.Sigmoid), R=[w2.b], W=[w2.b])
            kb.op("dve", lambda: nc.vector.tensor_scalar(out=w1[:], in0=w2[:], scalar1=-1.0, scalar2=1.0, op0=ALU.mult, op1=ALU.add),
                  R=[w2.b], W=[w1.b])
            kb.op("dve", lambda: nc.vector.tensor_scalar(out=cb[:], in0=k1[:], scalar1=w1[:, 0:1], scalar2=None, op0=ALU.mult),
                  R=[k1.b, w1.b], W=[cb.b])
            kb.op("dve", lambda: nc.vector.scalar_tensor_tensor(out=cb[:], in0=k2[:], scalar=w2[:, 0:1], in1=cb[:], op0=ALU.mult, op1=ALU.add),
                  R=[k2.b, w2.b, cb.b], W=[cb.b])
            ps2, pb2 = psum_bank(g)
            kb.op("pe", lambda: nc.tensor.transpose(ps2[0:NE, 0:128], cb[:], g.ident[:]), R=[cb.b, g.ident.b], W=[pb2])
            kb.op("act", lambda: nc.scalar.copy(out=c[:], in_=ps2[0:NE, 0:128]), R=[pb2], W=[c.b])
            kb.dma("sp", g.CombT[:, t0:t0 + 128], c[:], R=[c.b], sbuf=c.b)
        kb.barrier()


def stage_ffn(g, L, xsrc):
    kb, nc, T = g.kb, g.nc, g.T
    moe = not L.rwkv
    ne = NE if moe else 1
    TB = 512
    NF = DFF // 128
    with ExitStack() as st:
        xT = Tile(kb, st, "xT", [128, 16, TB], BF16)
        hT = Tile(kb, st, "hT", [128, NF, TB], BF16)
        wg = [Tile(kb, st, "wg", [128, 16, 128], BF16) for _ in range(2)]
        wu = [Tile(kb, st, "wu", [128, 16, 128], BF16) for _ in range(2)]
        wd = [Tile(kb, st, "wd", [128, NF, 128], BF16) for _ in range(2)]
        sg = [Tile(kb, st, "sg", [128, TB], F32) for _ in range(2)]
        xr = [Tile(kb, st, "xr", [128, TB], F32) for _ in range(2)]
        if moe:
            facc = Tile(kb, st, "facc", [128, 16, TB], F32)
            cbc = Tile(kb, st, "cbc", [128, NE, TB], F32)
        i2 = 0
        id_ = 0
        dG, dU, dD = {}, {}, {}
        for t0 in range(0, T, TB):
            kb.dma("pool", xT[:], act_T_ap(xsrc, 0, D, t0, TB), W=[xT.b], sbuf=xT.b)
            if moe:
                for e in range(NE):
                    kb.dma("sp", cbc[:, e, :], g.CombT[e:e + 1, t0:t0 + TB].partition_broadcast(128), W=[cbc.b], sbuf=cbc.b)
            for e in range(ne):
                for f in range(NF):
                    a = wg[i2]
                    b_ = wu[i2]
                    s_ = sg[i2]
                    i2 ^= 1
                    ix = e * NF + f
                    cG = L.wtG[ix, :, :].rearrange("p (kc n) -> p kc n", n=128)
                    cU = L.wtU[ix, :, :].rearrange("p (kc n) -> p kc n", n=128)
                    if t0 == 0:
                        dG[ix] = kb.buf("dG")
                        dU[ix] = kb.buf("dU")
                        kb.dma("pool", a[:], wtile_ap(L.ffn_gate, e * D, D, f * 128, 128), W=[a.b], sbuf=a.b)
                        kb.dma("pool", b_[:], wtile_ap(L.ffn_up, e * D, D, f * 128, 128), W=[b_.b], sbuf=b_.b)
                        if T > TB:
                            kb.dma("sp", cG, a[:], R=[a.b], W=[dG[ix]], sbuf=a.b)
                            kb.dma("sp", cU, b_[:], R=[b_.b], W=[dU[ix]], sbuf=b_.b)
                    else:
                        kb.dma("sp", a[:], cG, R=[dG[ix]], W=[a.b], sbuf=a.b)
                        kb.dma("sp", b_[:], cU, R=[dU[ix]], W=[b_.b], sbuf=b_.b)
                    psg, pbg = psum_bank(g)
                    for kc in range(16):
                        kb.op("pe", lambda kc=kc: nc.tensor.matmul(psg, a[:, kc, :], xT[:, kc, :], start=(kc == 0), stop=(kc == 15)),
                              R=[a.b, xT.b], W=[pbg])
                    psu, pbu = psum_bank(g)
                    for kc in range(16):
                        kb.op("pe", lambda kc=kc: nc.tensor.matmul(psu, b_[:, kc, :], xT[:, kc, :], start=(kc == 0), stop=(kc == 15)),
                              R=[b_.b, xT.b], W=[pbu])
                    kb.op("act", lambda: nc.scalar.activation(out=s_[:], in_=psg, func=AF.Silu), R=[pbg], W=[s_.b])
                    if moe:
                        kb.op("pool", lambda: nc.gpsimd.tensor_tensor(out=s_[:], in0=s_[:], in1=cbc[:, e, :], op=ALU.mult),
                              R=[s_.b, cbc.b], W=[s_.b])
                    kb.op("dve", lambda: nc.vector.tensor_tensor(out=hT[:, f, :], in0=s_[:], in1=psu, op=ALU.mult),
                          R=[s_.b, pbu], W=[hT.b])
                for dc in range(16):
                    w = wd[id_]
                    x_ = xr[id_]
                    id_ ^= 1
                    ixd = e * 16 + dc
                    cD = L.wtD[ixd, :, :].rearrange("p (kc n) -> p kc n", n=128)
                    if t0 == 0:
                        dD[ixd] = kb.buf("dD")
                        kb.dma("pool", w[:], wtile_ap(L.ffn_down, e * DFF, DFF, dc * 128, 128), W=[w.b], sbuf=w.b)
                        if T > TB:
                            kb.dma("sp", cD, w[:], R=[w.b], W=[dD[ixd]], sbuf=w.b)
                    else:
                        kb.dma("sp", w[:], cD, R=[dD[ixd]], W=[w.b], sbuf=w.b)
                    lastE = (e == ne - 1)
                    if lastE:
                        kb.dma("sp", x_[:], xsrc[dc * 128:(dc + 1) * 128, t0:t0 + TB], W=[x_.b], sbuf=x_.b)
                    ps, pb = psum_bank(g)
                    for f in range(NF):
                        kb.op("pe", lambda f=f: nc.tensor.matmul(ps, w[:, f, :], hT[:, f, :], start=(f == 0), stop=(f == NF - 1)),
                              R=[w.b, hT.b], W=[pb])
                    if moe:
                        if e == 0:
                            kb.op("act", lambda: nc.scalar.copy(out=facc[:, dc, :], in_=ps), R=[pb], W=[facc.b])
                        else:
                            kb.op("dve", lambda: nc.vector.tensor_tensor(out=facc[:, dc, :], in0=facc[:, dc, :], in1=ps, op=ALU.add),
                                  R=[pb, facc.b], W=[facc.b])
                        if lastE:
                            kb.op("dve", lambda: nc.vector.scalar_tensor_tensor(out=x_[:], in0=x_[:], scalar=ALPHA, in1=facc[:, dc, :],
                                                                                op0=ALU.mult, op1=ALU.add),
                                  R=[x_.b, facc.b], W=[x_.b])
                    else:
                        kb.op("dve", lambda: nc.vector.scalar_tensor_tensor(out=x_[:], in0=x_[:], scalar=ALPHA, in1=ps,
                                                                            op0=ALU.mult, op1=ALU.add),
                              R=[x_.b, pb], W=[x_.b])
                    if lastE:
                        kb.dma("sp", g.Zd[dc * 128:(dc + 1) * 128, t0:t0 + TB], x_[:], R=[x_.b], sbuf=x_.b)
        kb.barrier()


def stage_memattn(g, L):
    kb, nc, T, S = g.kb, g.nc, g.T, g.S
    q0 = L.C - MEMW
    scale = float(128 ** -0.5)
    with ExitStack() as st:
        qt = [Tile(kb, st, "q", [128, 4, 128], BF16) for _ in range(2)]
        ot = [Tile(kb, st, "o", [128, 4, 128], BF16) for _ in range(2)]
        mx = Tile(kb, st, "mx", [128, 1], F32)
        rs = Tile(kb, st, "rs", [128, 1], F32)
        ee = Tile(kb, st, "ee", [128, MEMLEN], F32)
        pp = Tile(kb, st, "pp", [128, MEMLEN], BF16)
        pT = Tile(kb, st, "pT", [128, 2, 128], BF16)
        pst_ = st.enter_context(nc.psum_tensor(kb.name("pstb"), [128, 2, 128], BF16)) if False else None
        qi = 0
        for t0 in range(0, T, 128):
            b = t0 // S
            q = qt[qi]
            o = ot[qi]
            qi ^= 1
            kb.dma("pool", q[:], g.Pd[q0:q0 + MEMW, t0:t0 + 128].rearrange("(h p) t -> p h t", p=128), W=[q.b], sbuf=q.b)
            for h in range(4):
                ps, pb = psum_bank(g)
                kb.op("pe", lambda: nc.tensor.matmul(ps[:, 0:MEMLEN], q[:, h, :], g.memk[:, h, b, :], start=True, stop=True),
                      R=[q.b, g.memk.b], W=[pb])
                kb.op("dve", lambda: nc.vector.reduce_max(out=mx[:], in_=ps[:, 0:MEMLEN], axis=AX.X), R=[pb], W=[mx.b])
                kb.op("dve", lambda: nc.vector.tensor_scalar(out=mx[:], in0=mx[:], scalar1=-scale, scalar2=None, op0=ALU.mult),
                      R=[mx.b], W=[mx.b])
                kb.op("act", lambda: nc.scalar.activation(out=ee[:], in_=ps[:, 0:MEMLEN], func=AF.Exp, bias=mx[:, 0:1], scale=scale,
                                                          accum_out=rs[:, 0:1]), R=[pb, mx.b], W=[ee.b, rs.b])
                kb.op("dve", lambda: nc.vector.reciprocal(out=rs[:], in_=rs[:]), R=[rs.b], W=[rs.b])
                kb.op("dve", lambda: nc.vector.tensor_scalar(out=pp[:], in0=ee[:], scalar1=rs[:, 0:1], scalar2=None, op0=ALU.mult),
                      R=[ee.b, rs.b], W=[pp.b])
                for mc in range(2):
                    ps2, pb2 = psum_bank(g)
                    ps2b = ps2.bitcast(BF16)
                    kb.op("pe", lambda: nc.tensor.transpose(ps2b[:, 0:128], pp[:, mc * 128:(mc + 1) * 128], g.identb[:]),
                          R=[pp.b, g.identb.b], W=[pb2])
                    kb.op("act", lambda: nc.scalar.copy(out=pT[:, mc, :], in_=ps2b[:, 0:128]), R=[pb2], W=[pT.b])
                ps3, pb3 = psum_bank(g)
                for mc in range(2):
                    kb.op("pe", lambda mc=mc: nc.tensor.matmul(ps3[:, 0:128], g.memv[:, b, mc, h * 128:(h + 1) * 128], pT[:, mc, :],
                                                               start=(mc == 0), stop=(mc == 1)),
                          R=[g.memv.b, pT.b], W=[pb3])
                kb.op("act", lambda: nc.scalar.copy(out=o[:, h, :], in_=ps3[:, 0:128]), R=[pb3], W=[o.b])
            kb.dma("sp", g.Ymix[SEQW:D, t0:t0 + 128].rearrange("(h p) t -> p h t", p=128), o[:], R=[o.b], sbuf=o.b)
        kb.barrier()


def _mix_tile(g, st_tiles, pt, m, mu_ap, TBp, out_ap, eng="dve"):
    kb, nc = g.kb, g.nc
    d = st_tiles["d"]
    kb.op("dve", lambda: nc.vector.tensor_tensor(out=d[0:m, :], in0=pt[0:m, 0:TBp], in1=pt[0:m, 1:TBp + 1], op=ALU.subtract),
          R=[st_tiles["ptb"]], W=[d.b])
    kb.op("dve", lambda: nc.vector.scalar_tensor_tensor(out=out_ap, in0=d[0:m, :], scalar=mu_ap, in1=pt[0:m, 1:TBp + 1],
                                                        op0=ALU.mult, op1=ALU.add),
          R=[d.b, st_tiles["ptb"], st_tiles["mub"]], W=[st_tiles["outb"]])


def stage_rwkv_prep(g, L):
    kb, nc, T, S, NB = g.kb, g.nc, g.T, g.S, g.NB
    TBp = min(512, S)
    withv = L.withv
    nv = 8 if withv else 7
    iw0, ia0 = 0, 1
    iv0 = 2 if withv else None
    ikk, ika, irk = (3, 4, 5) if withv else (2, 3, 4)
    cw, ca = 4608, 4704
    cv = 4800 if withv else None
    cg = 4864 if withv else 4800
    NEG_E = -float(np.exp(-0.5))
    with ExitStack() as st:
        mu = Tile(kb, st, "mu", [128, 41], F32)
        vec = Tile(kb, st, "vec", [128, nv * 12], F32)
        kb.dma("sp", mu[:], L.mu[:, :], W=[mu.b], sbuf=mu.b)
        kb.dma("sp", vec[:], L.vecs[:, :], W=[vec.b], sbuf=vec.b)
        lw = Tile(kb, st, "lw", [96, SEQW], BF16)
        la = Tile(kb, st, "la", [96, SEQW], BF16)
        lg = Tile(kb, st, "lg", [128, 2, SEQW], BF16)
        kb.dma("pool", lw[:], L.lora_w[:, :], W=[lw.b], sbuf=lw.b)
        kb.dma("pool", la[:], L.lora_a[:, :], W=[la.b], sbuf=la.b)
        kb.dma("pool", lg[:], L.lora_g[:, :].rearrange("(kc p) n -> p kc n", p=128), W=[lg.b], sbuf=lg.b)
        if withv:
            lv = Tile(kb, st, "lv", [64, SEQW], BF16)
            kb.dma("pool", lv[:], L.lora_v[:, :], W=[lv.b], sbuf=lv.b)
        d = Tile(kb, st, "d", [128, TBp], F32)
        sp_ = Tile(kb, st, "sp", [128, TBp + 1], F32)
        sm = Tile(kb, st, "sm", [128, TBp], F32)
        tw = Tile(kb, st, "tw", [96, TBp], BF16)
        ta = Tile(kb, st, "ta", [96, TBp], BF16)
        tv = Tile(kb, st, "tv", [64, TBp], BF16)
        tg = Tile(kb, st, "tg", [128, 2, TBp], BF16)
        NSET = 2
        def mk(nm, dt=F32, shape=None):
            return [Tile(kb, st, nm, shape or [128, TBp], dt) for _ in range(NSET)]
        rkv = mk("rkv", shape=[128, 3, TBp + 1])
        rm, km, vm = mk("rm"), mk("km"), mk("vm")
        dec, aa, gg, kk, sq, nkk, bq, kp, rk, bon, vf = (mk("dec"), mk("aa"), mk("gg"), mk("kk"), mk("sq"), mk("nkk"),
                                                        mk("bq"), mk("kp"), mk("rk"), mk("bon"), mk("vf"))
        vt = mk("vt", shape=[128, TBp // 128, 128])
        si = 0

        def load_shift(tile_ap_fn, tb_, rows0, m, t0, bstart, nseg=None):
            pass

        for b in range(NB):
            for tl0 in range(0, S, TBp):
                t0 = b * S + tl0
                first = (tl0 == 0)

                def ld(tile, pidx, r0, m, sub=None):
                    dst = (lambda a, bb: tile[0:m, a:bb]) if sub is None else (lambda a, bb: tile[0:m, sub, a:bb])
                    if first:
                        kb.op("pool", lambda: nc.gpsimd.memset(dst(0, 1), 0.0), W=[tile.b])
                        kb.dma("sp", dst(1, TBp + 1), g.Pd[r0:r0 + m, t0:t0 + TBp], W=[tile.b], sbuf=tile.b, group=False)
                    else:
                        kb.dma("sp", dst(0, TBp + 1), g.Pd[r0:r0 + m, t0 - 1:t0 + TBp], W=[tile.b], sbuf=tile.b)

                def mixs(m, mucol, out_ap, outb):
                    _mix_tile(g, {"d": d, "ptb": sp_.b, "mub": mu.b, "outb": outb}, sp_, m, mu[0:m, mucol:mucol + 1], TBp, out_ap)

                ld(sp_, 0, cw, 96)
                mixs(96, 36, sm[0:96, :], sm.b)
                kb.op("act", lambda: nc.scalar.activation(out=tw[:], in_=sm[0:96, :], func=AF.Tanh), R=[sm.b], W=[tw.b])
                ld(sp_, 0, ca, 96)
                mixs(96, 37, sm[0:96, :], sm.b)
                kb.op("act", lambda: nc.scalar.copy(out=ta[:], in_=sm[0:96, :]), R=[sm.b], W=[ta.b])
                if withv:
                    ld(sp_, 0, cv, 64)
                    mixs(64, 38, sm[0:64, :], sm.b)
                    kb.op("act", lambda: nc.scalar.copy(out=tv[:], in_=sm[0:64, :]), R=[sm.b], W=[tv.b])
                for kc in range(2):
                    ld(sp_, 0, cg + kc * 128, 128)
                    mixs(128, 39 + kc, sm[:, :], sm.b)
                    kb.op("act", lambda kc=kc: nc.scalar.activation(out=tg[:, kc, :], in_=sm[:, :], func=AF.Sigmoid), R=[sm.b], W=[tg.b])

                for c in range(12):
                    i = si
                    si = (si + 1) % NSET
                    R3 = rkv[i]
                    for j in range(3):
                        ld(R3, 0, j * SEQW + c * 128, 128, sub=j)
                    for j, dstt in enumerate((rm[i], km[i], vm[i])):
                        kb.op("dve", lambda j=j: nc.vector.tensor_tensor(out=d[:, :], in0=R3[:, j, 0:TBp], in1=R3[:, j, 1:TBp + 1], op=ALU.subtract),
                              R=[R3.b], W=[d.b])
                        kb.op("dve", lambda j=j, dstt=dstt: nc.vector.scalar_tensor_tensor(
                            out=dstt[:], in0=d[:, :], scalar=mu[:, j * 12 + c:j * 12 + c + 1], in1=R3[:, j, 1:TBp + 1],
                            op0=ALU.mult, op1=ALU.add), R=[d.b, R3.b, mu.b], W=[dstt.b])
                    cs = slice(c * 128, (c + 1) * 128)

                    def vcol(iv):
                        return vec[:, iv * 12 + c:iv * 12 + c + 1]
                    ps, pb = psum_bank(g)
                    kb.op("pe", lambda: nc.tensor.matmul(ps[:, 0:TBp], lw[:, cs], tw[:], start=True, stop=True), R=[lw.b, tw.b], W=[pb])
                    kb.op("act", lambda: nc.scalar.activation(out=dec[i][:], in_=ps[:, 0:TBp], func=AF.Sigmoid, bias=vcol(iw0)),
                          R=[pb, vec.b], W=[dec[i].b])
                    kb.op("act", lambda: nc.scalar.activation(out=dec[i][:], in_=dec[i][:], func=AF.Exp, scale=NEG_E),
                          R=[dec[i].b], W=[dec[i].b])
                    kb.dma("sp", g.Q[1][b, cs, tl0:tl0 + TBp], dec[i][:], R=[dec[i].b], sbuf=dec[i].b)
                    ps, pb = psum_bank(g)
                    kb.op("pe", lambda: nc.tensor.matmul(ps[:, 0:TBp], la[:, cs], ta[:], start=True, stop=True), R=[la.b, ta.b], W=[pb])
                    kb.op("act", lambda: nc.scalar.activation(out=aa[i][:], in_=ps[:, 0:TBp], func=AF.Sigmoid, bias=vcol(ia0)),
                          R=[pb, vec.b], W=[aa[i].b])
                    ps, pb = psum_bank(g)
                    for kc in range(2):
                        kb.op("pe", lambda kc=kc: nc.tensor.matmul(ps[:, 0:TBp], lg[:, kc, cs], tg[:, kc, :], start=(kc == 0), stop=(kc == 1)),
                              R=[lg.b, tg.b], W=[pb])
                    kb.op("act", lambda: nc.scalar.copy(out=gg[i][:], in_=ps[:, 0:TBp]), R=[pb], W=[gg[i].b])
                    kb.dma("sp", g.Gd[cs, t0:t0 + TBp], gg[i][:], R=[gg[i].b], sbuf=gg[i].b)
                    if withv:
                        ps, pb = psum_bank(g)
                        kb.op("pe", lambda: nc.tensor.matmul(ps[:, 0:TBp], lv[:, cs], tv[:], start=True, stop=True), R=[lv.b, tv.b], W=[pb])
                        kb.op("act", lambda: nc.scalar.activation(out=sq[i][:], in_=ps[:, 0:TBp], func=AF.Sigmoid, bias=vcol(iv0)),
                              R=[pb, vec.b], W=[sq[i].b])
                        kb.dma("sp", vf[i][:], g.Vfirst[cs, t0:t0 + TBp], W=[vf[i].b], sbuf=vf[i].b)
                        kb.op("dve", lambda: nc.vector.tensor_tensor(out=vf[i][:], in0=vf[i][:], in1=vm[i][:], op=ALU.subtract),
                              R=[vf[i].b, vm[i].b], W=[vf[i].b])
                        kb.op("dve", lambda: nc.vector.tensor_tensor(out=vf[i][:], in0=vf[i][:], in1=sq[i][:], op=ALU.mult),
                              R=[vf[i].b, sq[i].b], W=[vf[i].b])
                        kb.op("dve", lambda: nc.vector.tensor_tensor(out=vm[i][:], in0=vm[i][:], in1=vf[i][:], op=ALU.add),
                              R=[vf[i].b, vm[i].b], W=[vm[i].b])
                    else:
                        kb.dma("sp", g.Vfirst[cs, t0:t0 + TBp], vm[i][:], R=[vm[i].b], sbuf=vm[i].b)
                    kb.op("dve", lambda: nc.vector.tensor_scalar(out=kk[i][:], in0=km[i][:], scalar1=vcol(ikk), scalar2=None, op0=ALU.mult),
                          R=[km[i].b, vec.b], W=[kk[i].b])
                    kb.op("act", lambda: nc.scalar.activation(out=sq[i][:], in_=kk[i][:], func=AF.Square), R=[kk[i].b], W=[sq[i].b])
                    ps, pb = psum_bank(g)
                    kb.op("pe", lambda: nc.tensor.matmul(ps[:, 0:TBp], g.blk[:], sq[i][:], start=True, stop=True), R=[g.blk.b, sq[i].b], W=[pb])
                    kb.op("act", lambda: nc.scalar.sqrt(out=sq[i][:], in_=ps[:, 0:TBp]), R=[pb], W=[sq[i].b])
                    kb.op("dve", lambda: nc.vector.tensor_scalar(out=sq[i][:], in0=sq[i][:], scalar1=1e-12, scalar2=None, op0=ALU.max),
                          R=[sq[i].b], W=[sq[i].b])
                    kb.op("dve", lambda: nc.vector.reciprocal(out=sq[i][:], in_=sq[i][:]), R=[sq[i].b], W=[sq[i].b])
                    kb.op("dve", lambda: nc.vector.scalar_tensor_tensor(out=nkk[i][:], in0=kk[i][:], scalar=-1.0, in1=sq[i][:],
                                                                        op0=ALU.mult, op1=ALU.mult), R=[kk[i].b, sq[i].b], W=[nkk[i].b])
                    kb.dma("sp", g.Q[3][b, cs, tl0:tl0 + TBp], nkk[i][:], R=[nkk[i].b], sbuf=nkk[i].b)
                    kb.op("dve", lambda: nc.vector.scalar_tensor_tensor(out=bq[i][:], in0=nkk[i][:], scalar=-1.0, in1=aa[i][:],
                                                                        op0=ALU.mult, op1=ALU.mult), R=[nkk[i].b, aa[i].b], W=[bq[i].b])
                    kb.dma("sp", g.Q[4][b, cs, tl0:tl0 + TBp], bq[i][:], R=[bq[i].b], sbuf=bq[i].b)
                    kb.op("dve", lambda: nc.vector.tensor_scalar(out=kp[i][:], in0=aa[i][:], scalar1=-1.0, scalar2=vcol(ika),
                                                                 op0=ALU.add, op1=ALU.mult), R=[aa[i].b, vec.b], W=[kp[i].b])
                    kb.op("dve", lambda: nc.vector.scalar_tensor_tensor(out=kp[i][:], in0=kp[i][:], scalar=1.0, in1=km[i][:],
                                                                        op0=ALU.add, op1=ALU.mult), R=[kp[i].b, km[i].b], W=[kp[i].b])
                    kb.dma("sp", g.Q[2][b, cs, tl0:tl0 + TBp], kp[i][:], R=[kp[i].b], sbuf=kp[i].b)
                    kb.dma("sp", g.Q[0][b, cs, tl0:tl0 + TBp], rm[i][:], R=[rm[i].b], sbuf=rm[i].b)
                    kb.op("dve", lambda: nc.vector.scalar_tensor_tensor(out=rk[i][:], in0=rm[i][:], scalar=vcol(irk), in1=kp[i][:],
                                                                        op0=ALU.mult, op1=ALU.mult), R=[rm[i].b, kp[i].b, vec.b], W=[rk[i].b])
                    ps, pb = psum_bank(g)
                    kb.op("pe", lambda: nc.tensor.matmul(ps[:, 0:TBp], g.blk[:], rk[i][:], start=True, stop=True), R=[g.blk.b, rk[i].b], W=[pb])
                    kb.op("dve", lambda: nc.vector.tensor_tensor(out=bon[i][:], in0=vm[i][:], in1=ps[:, 0:TBp], op=ALU.mult),
                          R=[pb, vm[i].b], W=[bon[i].b])
                    kb.dma("sp", g.Bonus[cs, t0:t0 + TBp], bon[i][:], R=[bon[i].b], sbuf=bon[i].b)
                    for q in range(TBp // 128):
                        ps, pb = psum_bank(g)
                        kb.op("pe", lambda q=q: nc.tensor.transpose(ps[:, 0:128], vm[i][:, q * 128:(q + 1) * 128], g.ident[:]),
                              R=[vm[i].b, g.ident.b], W=[pb])
                        kb.op("act", lambda q=q: nc.scalar.copy(out=vt[i][:, q, :], in_=ps[:, 0:128]), R=[pb], W=[vt[i].b])
                    kb.dma("sp", g.Vtok[b, tl0:tl0 + TBp, cs].rearrange("(q p) c -> p q c", p=128), vt[i][:], R=[vt[i].b], sbuf=vt[i].b)
        kb.barrier()


def stage_rwkv_scan(g, L):
    kb, nc, T, S, NB = g.kb, g.nc, g.T, g.S, g.NB
    TS = 32
    TSV = 4
    with ExitStack() as st:
        ysel = Tile(kb, st, "ysel", [128, TS, 64], F32)
        kb.dma("sp", ysel[:], g.din["c_ysel"][:, :].rearrange("p (t m) -> p t m", m=64), W=[ysel.b], sbuf=ysel.b)
        qin = [[Tile(kb, st, "qin%d" % k, [128, NH, TS], F32) for k in range(5)] for _ in range(2)]
        vbc = [Tile(kb, st, "vbc", [128, TSV, SEQW], F32) for _ in range(2)]
        H = Tile(kb, st, "H", [128, SEQW], F32)
        A = Tile(kb, st, "A", [128, SEQW], F32)
        T1 = Tile(kb, st, "T1", [128, SEQW], F32)
        T2 = Tile(kb, st, "T2", [128, SEQW], F32)
        T3 = Tile(kb, st, "T3", [128, SEQW], F32)
        T4 = Tile(kb, st, "T4", [128, SEQW], F32)
        yt = [Tile(kb, st, "yt", [64, SEQW], F32) for _ in range(2)]
        kb.op("dve", lambda: nc.vector.memset(H[:], 0.0), W=[H.b])
        yb = [g.pb[0], g.pb[1], g.pb[2]]
        ub = [[g.pb[3], g.pb[4], g.pb[5]]]
        yacc = g.ps[0:64, 0:3, :]
        upsl = [g.ps[:, 3:6, :]]

        def v3(ap):
            return ap.rearrange("p (h i) -> p h i", i=64)

        def bc(tile, tl):
            return tile[:, :, tl:tl + 1].to_broadcast([128, NH, 64])

        T3s = [T3, Tile(kb, st, "T3b", [128, SEQW], F32)]
        vtiles = {}

        def load_v(tglob):
            vb_ = vbc[(tglob // TSV) % 2]
            for b in range(NB):
                src = g.Vtok[b:b + 1, tglob:tglob + TSV, :].rearrange("o t c -> o (t c)").partition_broadcast(64)
                kb.dma("sp", vb_[b * 64:(b + 1) * 64, :, :].rearrange("p t c -> p (t c)"), src, W=[vb_.b], sbuf=vb_.b)
            vtiles[tglob // TSV] = vb_

        def emit_T3(tglob, qk, tl):
            vb_ = vtiles[tglob // TSV]
            t3 = T3s[tglob % 2]
            kb.op("pool", lambda: nc.gpsimd.tensor_tensor(out=v3(t3[:]), in0=v3(vb_[:, tglob % TSV, :]), in1=bc(qk, tl), op=ALU.mult),
                  R=[vb_.b, qk.b], W=[t3.b])

        def emit_y(pv):
            qr_p, tl_p, blk_p, y_p = pv
            kb.op("dve", lambda: nc.vector.tensor_tensor(out=v3(T4[:]), in0=v3(H[:]), in1=bc(qr_p, tl_p), op=ALU.mult),
                  R=[H.b, qr_p.b], W=[T4.b])
            yl = ysel[:, tl_p, :]
            for j in range(3):
                kb.op("pe", lambda j=j: nc.tensor.matmul(yacc[:, j, :], yl, T4[:, j * 512:(j + 1) * 512], start=(tl_p == 0), stop=(tl_p == TS - 1)),
                      R=[ysel.b, T4.b], W=[yb[j]])
            if tl_p == TS - 1:
                kb.op("act", lambda: nc.scalar.copy(out=y_p[:], in_=yacc.rearrange("p a n -> p (a n)")), R=yb, W=[y_p.b])
                for b in range(NB):
                    kb.dma("sp", g.Ytok[b, blk_p:blk_p + TS, :], y_p[b * TS:(b + 1) * TS, :], R=[y_p.b], sbuf=y_p.b)

        bi = 0
        prev = None
        ups = upsl[0]
        ubb = ub[0]
        for blk0 in range(0, S, TS):
            qs = qin[bi]
            y_ = yt[bi]
            bi ^= 1
            for b in range(NB):
                for k in range(5):
                    kb.dma("sp", qs[k][b * 64:(b + 1) * 64, :, :],
                           g.Q[k][b, :, blk0:blk0 + TS].rearrange("(h j) t -> j h t", j=64), W=[qs[k].b], sbuf=qs[k].b)
            qr, qw, qk, qn, qb = qs
            for tl in range(TS):
                tg = blk0 + tl
                if tg % TSV == 0:
                    load_v(tg)
                kb.op("pool", lambda: nc.gpsimd.tensor_tensor(out=v3(A[:]), in0=v3(H[:]), in1=bc(qw, tl), op=ALU.mult),
                      R=[H.b, qw.b], W=[A.b])
                if tl == 0:
                    emit_T3(tg, qk, tl)
                kb.op("dve", lambda: nc.vector.tensor_tensor(out=v3(T1[:]), in0=v3(H[:]), in1=bc(qn, tl), op=ALU.mult),
                      R=[H.b, qn.b], W=[T1.b])
                for j in range(3):
                    kb.op("pe", lambda j=j: nc.tensor.matmul(ups[:, j, :], g.blk[:], T1[:, j * 512:(j + 1) * 512], start=True, stop=True),
                          R=[g.blk.b, T1.b], W=[ubb[j]])
                if prev is not None:
                    emit_y(prev)
                if tl + 1 < TS:
                    if (tg + 1) % TSV == 0:
                        load_v(tg + 1)
                    emit_T3(tg + 1, qk, tl + 1)
                t3 = T3s[tg % 2]
                kb.op("dve", lambda: nc.vector.tensor_tensor(out=v3(T2[:]), in0=ups.rearrange("p a (h i) -> p (a h) i", i=64),
                                                             in1=bc(qb, tl), op=ALU.mult), R=ubb + [qb.b], W=[T2.b])
                kb.op("dve", lambda: nc.vector.tensor_tensor(out=A[:], in0=A[:], in1=t3[:], op=ALU.add), R=[A.b, t3.b], W=[A.b])
                kb.op("dve", lambda: nc.vector.tensor_tensor(out=H[:], in0=A[:], in1=T2[:], op=ALU.add), R=[A.b, T2.b], W=[H.b])
                prev = (qr, tl, blk0, y_)
        emit_y(prev)
        kb.barrier()


def stage_rwkv_post(g, L):
    kb, nc, T, S, NB = g.kb, g.nc, g.T, g.S, g.NB
    nv = 8 if L.withv else 7
    ign, igb = nv - 2, nv - 1
    with ExitStack() as st:
        vec = Tile(kb, st, "vec", [128, nv * 12], F32)
        kb.dma("sp", vec[:], L.vecs[:, :], W=[vec.b], sbuf=vec.b)
        ytk = [Tile(kb, st, "ytk", [128, SEQW], F32) for _ in range(2)]
        sqt = Tile(kb, st, "sqt", [128, SEQW], F32)
        s1 = Tile(kb, st, "s1", [128, NH], F32)
        s2 = Tile(kb, st, "s2", [128, NH], F32)
        bon = [Tile(kb, st, "bon", [128, 12, 128], F32) for _ in range(2)]
        gt = [Tile(kb, st, "gt", [128, 12, 128], F32) for _ in range(2)]
        fm = Tile(kb, st, "fm", [128, 12, 128], F32)
        ob = [Tile(kb, st, "ob", [128, 12, 128], BF16) for _ in range(2)]

        def v3(ap):
            return ap.rearrange("p (h i) -> p h i", i=64)
        i = 0
        for t0 in range(0, T, 128):
            b = t0 // S
            tl0 = t0 - b * S
            y = ytk[i]
            bo = bon[i]
            gq = gt[i]
            o = ob[i]
            i ^= 1
            kb.dma("sp", y[:], g.Ytok[b, tl0:tl0 + 128, :], W=[y.b], sbuf=y.b)
            kb.dma("sp", bo[:], g.Bonus[:, t0:t0 + 128].rearrange("(c p) t -> p c t", p=128), W=[bo.b], sbuf=bo.b)
            kb.dma("sp", gq[:], g.Gd[:, t0:t0 + 128].rearrange("(c p) t -> p c t", p=128), W=[gq.b], sbuf=gq.b)
            kb.op("dve", lambda: nc.vector.tensor_reduce(out=s1[:], in_=v3(y[:]), axis=AX.X, op=ALU.add), R=[y.b], W=[s1.b])
            kb.op("dve", lambda: nc.vector.tensor_scalar(out=s1[:], in0=s1[:], scalar1=1.0 / 64, scalar2=None, op0=ALU.mult), R=[s1.b], W=[s1.b])
            kb.op("dve", lambda: nc.vector.tensor_tensor(out=v3(y[:]), in0=v3(y[:]), in1=s1[:].unsqueeze(2).to_broadcast([128, NH, 64]),
                                                         op=ALU.subtract), R=[y.b, s1.b], W=[y.b])
            kb.op("act", lambda: nc.scalar.activation(out=sqt[:], in_=y[:], func=AF.Square), R=[y.b], W=[sqt.b])
            kb.op("dve", lambda: nc.vector.tensor_reduce(out=s2[:], in_=v3(sqt[:]), axis=AX.X, op=ALU.add), R=[sqt.b], W=[s2.b])
            kb.op("dve", lambda: nc.vector.tensor_scalar(out=s2[:], in0=s2[:], scalar1=1.0 / 64, scalar2=GN_EPS_RWKV, op0=ALU.mult, op1=ALU.add),
                  R=[s2.b], W=[s2.b])
            kb.op("act", lambda: nc.scalar.sqrt(out=s2[:], in_=s2[:]), R=[s2.b], W=[s2.b])
            kb.op("dve", lambda: nc.vector.reciprocal(out=s2[:], in_=s2[:]), R=[s2.b], W=[s2.b])
            kb.op("dve", lambda: nc.vector.tensor_tensor(out=v3(y[:]), in0=v3(y[:]), in1=s2[:].unsqueeze(2).to_broadcast([128, NH, 64]),
                                                         op=ALU.mult), R=[y.b, s2.b], W=[y.b])
            for c in range(12):
                ps, pb = psum_bank(g)
                kb.op("pe", lambda c=c: nc.tensor.transpose(ps[:, 0:128], y[:, c * 128:(c + 1) * 128], g.ident[:]), R=[y.b, g.ident.b], W=[pb])
                kb.op("act", lambda c=c: nc.scalar.activation(out=fm[:, c, :], in_=ps[:, 0:128], func=AF.Identity,
                                                              scale=vec[:, ign * 12 + c:ign * 12 + c + 1],
                                                              bias=vec[:, igb * 12 + c:igb * 12 + c + 1]), R=[pb, vec.b], W=[fm.b])
            kb.op("dve", lambda: nc.vector.tensor_tensor(out=fm[:], in0=fm[:], in1=bo[:], op=ALU.add), R=[fm.b, bo.b], W=[fm.b])
            kb.op("dve", lambda: nc.vector.tensor_tensor(out=o[:], in0=fm[:], in1=gq[:], op=ALU.mult), R=[fm.b, gq.b], W=[o.b])
            kb.dma("sp", g.Ymix[0:SEQW, t0:t0 + 128].rearrange("(c p) t -> p c t", p=128), o[:], R=[o.b], sbuf=o.b)
        kb.barrier()


def stage_retention(g, L):
    kb, nc, T, S, NB = g.kb, g.nc, g.T, g.S, g.NB
    PI = float(np.pi)
    scale = float(128 ** -0.5)
    with ExitStack() as st:
        gn = Tile(kb, st, "gn", [128, 24], F32)
        rot = Tile(kb, st, "rot", [128, 4], F32)
        retT = Tile(kb, st, "retT", [128, RET_H, 128], F32)
        retq = Tile(kb, st, "retq", [128, RET_H, 128], F32)
        retk = Tile(kb, st, "retk", [128, RET_H], F32)
        retc = Tile(kb, st, "retc", [128, RET_H], F32)
        kb.dma("sp", gn[:], L.gn[:, :], W=[gn.b], sbuf=gn.b)
        kb.dma("sp", rot[:], g.din["c_rot"][:, :], W=[rot.b], sbuf=rot.b)
        kb.dma("sp", retT[:], g.din["c_retT"][:, :].rearrange("p (h n) -> p h n", n=128), W=[retT.b], sbuf=retT.b)
        kb.dma("sp", retq[:], g.din["c_retq"][:, :].rearrange("p (h n) -> p h n", n=128), W=[retq.b], sbuf=retq.b)
        kb.dma("sp", retk[:], g.din["c_retk"][:, :], W=[retk.b], sbuf=retk.b)
        kb.dma("sp", retc[:], g.din["c_retc"][:, :], W=[retc.b], sbuf=retc.b)
        QK = [Tile(kb, st, "QK", [128, 12, 128], F32) for _ in range(2)]
        VV = [Tile(kb, st, "VV", [128, 12, 128], F32) for _ in range(2)]
        GG = [Tile(kb, st, "GG", [128, 12, 128], F32) for _ in range(2)]
        OB = [Tile(kb, st, "OB", [128, 12, 128], BF16) for _ in range(2)]
        posi = Tile(kb, st, "posi", [128, 128], I32)
        ang = Tile(kb, st, "ang", [128, 128], F32)
        cs = Tile(kb, st, "cs", [128, 128], F32)
        sn = Tile(kb, st, "sn", [128, 128], F32)
        csk = Tile(kb, st, "csk", [128, 128], F32)
        snk = Tile(kb, st, "snk", [128, 128], F32)
        t1 = Tile(kb, st, "t1", [128, 128], F32)
        t2 = Tile(kb, st, "t2", [128, 128], F32)
        qr = Tile(kb, st, "qr", [128, 128], F32)
        kr = Tile(kb, st, "kr", [128, 128], F32)
        qb = Tile(kb, st, "qb", [128, 128], BF16)
        kbb = Tile(kb, st, "kbb", [128, 128], BF16)
        qd = Tile(kb, st, "qd", [128, 128], BF16)
        sTd = Tile(kb, st, "sTd", [128, 128], BF16)
        ktd = Tile(kb, st, "ktd", [128, 128], BF16)
        vtk = Tile(kb, st, "vtk", [128, 256], BF16)
        oo = Tile(kb, st, "oo", [128, 2, 128], F32)
        sq = Tile(kb, st, "sq", [128, 2, 128], F32)
        mean = Tile(kb, st, "mean", [128, 128], F32)
        rstd = Tile(kb, st, "rstd", [128, 128], F32)
        sg = Tile(kb, st, "sg", [128, 128], F32)
        Rs = [Tile(kb, st, "R", [128, 256], F32) for _ in range(RET_H)]
        Rb = [Tile(kb, st, "Rb", [128, 256], BF16) for _ in range(RET_H)]
        i = 0
        for t0 in range(0, T, 128):
            b = t0 // S
            if t0 % S == 0:
                for h in range(RET_H):
                    kb.op("dve", lambda h=h: nc.vector.memset(Rs[h][:], 0.0), W=[Rs[h].b])
                    kb.op("dve", lambda h=h: nc.vector.memset(Rb[h][:], 0.0), W=[Rb[h].b])
            X, V, Gt, O = QK[i], VV[i], GG[i], OB[i]
            i ^= 1
            kb.dma("sp", X[:], g.Pd[0:1536, t0:t0 + 128].rearrange("(c p) t -> p c t", p=128), W=[X.b], sbuf=X.b)
            kb.dma("sp", V[:], g.Pd[1536:3072, t0:t0 + 128].rearrange("(c p) t -> p c t", p=128), W=[V.b], sbuf=V.b)
            kb.dma("sp", Gt[:], g.Pd[3072:4608, t0:t0 + 128].rearrange("(c p) t -> p c t", p=128), W=[Gt.b], sbuf=Gt.b)
            kb.dma("sp", posi[:], g.din["pos"][0:1, t0:t0 + 128].partition_broadcast(128), W=[posi.b], sbuf=posi.b)
            kb.op("dve", lambda: nc.vector.tensor_copy(out=ang[:], in_=posi[:]), R=[posi.b], W=[ang.b])
            kb.op("dve", lambda: nc.vector.tensor_scalar(out=ang[:], in0=ang[:], scalar1=rot[:, 0:1], scalar2=None, op0=ALU.mult),
                  R=[ang.b, rot.b], W=[ang.b])
            C1 = 6.28125
            C2 = float(2 * np.pi - 6.28125)
            for dst, off in ((sn, 0.0), (cs, 0.25)):
                kb.op("dve", lambda off=off: nc.vector.tensor_scalar(out=t1[:], in0=ang[:], scalar1=float(1.0 / (2 * np.pi)), scalar2=off,
                                                                     op0=ALU.mult, op1=ALU.add), R=[ang.b], W=[t1.b])
                kb.op("dve", lambda: nc.vector.tensor_copy(out=posi[:], in_=t1[:]), R=[t1.b], W=[posi.b])
                kb.op("dve", lambda: nc.vector.tensor_copy(out=t2[:], in_=posi[:]), R=[posi.b], W=[t2.b])
                kb.op("dve", lambda: nc.vector.tensor_tensor(out=t1[:], in0=t1[:], in1=t2[:], op=ALU.subtract), R=[t1.b, t2.b], W=[t1.b])
                kb.op("dve", lambda: nc.vector.tensor_scalar(out=t1[:], in0=t1[:], scalar1=0.5, scalar2=None, op0=ALU.is_ge), R=[t1.b], W=[t1.b])
                kb.op("dve", lambda: nc.vector.tensor_tensor(out=t2[:], in0=t2[:], in1=t1[:], op=ALU.add), R=[t1.b, t2.b], W=[t2.b])
                kb.op("dve", lambda dst=dst: nc.vector.scalar_tensor_tensor(out=dst[:], in0=t2[:], scalar=-C1, in1=ang[:],
                                                                            op0=ALU.mult, op1=ALU.add), R=[t2.b, ang.b], W=[dst.b])
                kb.op("dve", lambda dst=dst: nc.vector.scalar_tensor_tensor(out=dst[:], in0=t2[:], scalar=-C2, in1=dst[:],
                                                                            op0=ALU.mult, op1=ALU.add), R=[t2.b, dst.b], W=[dst.b])
                if off:
                    kb.op("dve", lambda dst=dst: nc.vector.tensor_scalar(out=dst[:], in0=dst[:], scalar1=float(np.pi / 2), scalar2=None,
                                                                         op0=ALU.add), R=[dst.b], W=[dst.b])
                kb.op("dve", lambda dst=dst: nc.vector.tensor_scalar(out=dst[:], in0=dst[:], scalar1=-PI, scalar2=PI, op0=ALU.max, op1=ALU.min),
                      R=[dst.b], W=[dst.b])
                kb.op("act", lambda dst=dst: nc.scalar.activation(out=dst[:], in_=dst[:], func=AF.Sin), R=[dst.b], W=[dst.b])
            kb.op("dve", lambda: nc.vector.tensor_scalar(out=sn[:], in0=sn[:], scalar1=rot[:, 1:2], scalar2=None, op0=ALU.mult),
                  R=[sn.b, rot.b], W=[sn.b])
            kb.op("dve", lambda: nc.vector.tensor_scalar(out=csk[:], in0=cs[:], scalar1=scale, scalar2=None, op0=ALU.mult), R=[cs.b], W=[csk.b])
            kb.op("dve", lambda: nc.vector.tensor_scalar(out=snk[:], in0=sn[:], scalar1=scale, scalar2=None, op0=ALU.mult), R=[sn.b], W=[snk.b])
            for h in range(RET_H):
                for (ch, c_t, s_t, dstt) in ((h, cs, sn, qr), (6 + h, csk, snk, kr)):
                    ps, pb = psum_bank(g)
                    kb.op("pe", lambda ch=ch: nc.tensor.matmul(ps[:, 0:128], g.swap[:], X[:, ch, :], start=True, stop=True),
                          R=[g.swap.b, X.b], W=[pb])
                    kb.op("dve", lambda ch=ch, c_t=c_t: nc.vector.tensor_tensor(out=t1[:], in0=X[:, ch, :], in1=c_t[:], op=ALU.mult),
                          R=[X.b, c_t.b], W=[t1.b])
                    kb.op("dve", lambda s_t=s_t: nc.vector.tensor_tensor(out=t2[:], in0=s_t[:], in1=ps[:, 0:128], op=ALU.mult),
                          R=[pb, s_t.b], W=[t2.b])
                    kb.op("dve", lambda dstt=dstt: nc.vector.tensor_tensor(out=dstt[:], in0=t1[:], in1=t2[:], op=ALU.add),
                          R=[t1.b, t2.b], W=[dstt.b])
                kb.op("act", lambda: nc.scalar.copy(out=qb[:], in_=qr[:]), R=[qr.b], W=[qb.b])
                kb.op("act", lambda: nc.scalar.copy(out=kbb[:], in_=kr[:]), R=[kr.b], W=[kbb.b])
                kb.op("dve", lambda: nc.vector.tensor_tensor(out=qd[:], in0=qr[:], in1=retq[:, h, :], op=ALU.mult), R=[qr.b, retq.b], W=[qd.b])
                ps, pb = psum_bank(g)
                kb.op("pe", lambda: nc.tensor.matmul(ps[:, 0:128], kbb[:], qb[:], start=True, stop=True), R=[kbb.b, qb.b], W=[pb])
                kb.op("dve", lambda: nc.vector.tensor_tensor(out=sTd[:], in0=retT[:, h, :], in1=ps[:, 0:128], op=ALU.mult),
                      R=[pb, retT.b], W=[sTd.b])
                ps, pb = psum_bank(g)
                kb.op("pe", lambda: nc.tensor.transpose(ps[:, 0:128], kr[:], g.ident[:]), R=[kr.b, g.ident.b], W=[pb])
                kb.op("act", lambda: nc.scalar.activation(out=ktd[:], in_=ps[:, 0:128], func=AF.Copy, scale=retk[:, h:h + 1]),
                      R=[pb, retk.b], W=[ktd.b])
                for ec in range(2):
                    ps, pb = psum_bank(g)
                    kb.op("pe", lambda ec=ec: nc.tensor.transpose(ps[:, 0:128], V[:, h * 2 + ec, :], g.ident[:]), R=[V.b, g.ident.b], W=[pb])
                    kb.op("act", lambda ec=ec: nc.scalar.copy(out=vtk[:, ec * 128:(ec + 1) * 128], in_=ps[:, 0:128]), R=[pb], W=[vtk.b])
                for ec in range(2):
                    ps, pb = psum_bank(g)
                    kb.op("pe", lambda ec=ec: nc.tensor.matmul(ps[:, 0:128], vtk[:, ec * 128:(ec + 1) * 128], sTd[:], start=True, stop=False),
                          R=[vtk.b, sTd.b], W=[pb])
                    kb.op("pe", lambda ec=ec: nc.tensor.matmul(ps[:, 0:128], Rb[h][:, ec * 128:(ec + 1) * 128], qd[:], start=False, stop=True),
                          R=[Rb[h].b, qd.b], W=[pb])
                    kb.op("act", lambda ec=ec: nc.scalar.copy(out=oo[:, ec, :], in_=ps[:, 0:128]), R=[pb], W=[oo.b])
                ps, pb = psum_bank(g)
                kb.op("pe", lambda: nc.tensor.matmul(ps[:, 0:256], ktd[:], vtk[:], start=True, stop=True), R=[ktd.b, vtk.b], W=[pb])
                kb.op("dve", lambda: nc.vector.scalar_tensor_tensor(out=Rs[h][:], in0=Rs[h][:], scalar=retc[:, h:h + 1], in1=ps[:, 0:256],
                                                                    op0=ALU.mult, op1=ALU.add), R=[Rs[h].b, retc.b, pb], W=[Rs[h].b])
                kb.op("act", lambda: nc.scalar.copy(out=Rb[h][:], in_=Rs[h][:]), R=[Rs[h].b], W=[Rb[h].b])
                ps, pb = psum_bank(g)
                for ec in range(2):
                    kb.op("pe", lambda ec=ec: nc.tensor.matmul(ps[:, 0:128], g.ones[:], oo[:, ec, :], start=(ec == 0), stop=(ec == 1)),
                          R=[g.ones.b, oo.b], W=[pb])
                kb.op("act", lambda: nc.scalar.mul(out=mean[:], in_=ps[:, 0:128], mul=1.0 / 256), R=[pb], W=[mean.b])
                kb.op("dve", lambda: nc.vector.tensor_tensor(out=oo[:], in0=oo[:], in1=mean[:].unsqueeze(1).to_broadcast([128, 2, 128]),
                                                             op=ALU.subtract), R=[oo.b, mean.b], W=[oo.b])
                kb.op("act", lambda: nc.scalar.activation(out=sq[:], in_=oo[:], func=AF.Square), R=[oo.b], W=[sq.b])
                ps, pb = psum_bank(g)
                for ec in range(2):
                    kb.op("pe", lambda ec=ec: nc.tensor.matmul(ps[:, 0:128], g.ones[:], sq[:, ec, :], start=(ec == 0), stop=(ec == 1)),
                          R=[g.ones.b, sq.b], W=[pb])
                kb.op("dve", lambda: nc.vector.tensor_scalar(out=rstd[:], in0=ps[:, 0:128], scalar1=1.0 / 256, scalar2=LN_EPS,
                                                             op0=ALU.mult, op1=ALU.add), R=[pb], W=[rstd.b])
                kb.op("act", lambda: nc.scalar.sqrt(out=rstd[:], in_=rstd[:]), R=[rstd.b], W=[rstd.b])
                kb.op("dve", lambda: nc.vector.reciprocal(out=rstd[:], in_=rstd[:]), R=[rstd.b], W=[rstd.b])
                kb.op("dve", lambda: nc.vector.tensor_tensor(out=oo[:], in0=oo[:], in1=rstd[:].unsqueeze(1).to_broadcast([128, 2, 128]),
                                                             op=ALU.mult), R=[oo.b, rstd.b], W=[oo.b])
                for ec in range(2):
                    cix = h * 2 + ec
                    kb.op("act", lambda ec=ec, cix=cix: nc.scalar.activation(out=oo[:, ec, :], in_=oo[:, ec, :], func=AF.Identity,
                                                                             scale=gn[:, cix:cix + 1], bias=gn[:, 12 + cix:12 + cix + 1]),
                          R=[oo.b, gn.b], W=[oo.b])
                    kb.op("act", lambda cix=cix: nc.scalar.activation(out=sg[:], in_=Gt[:, cix, :], func=AF.Silu), R=[Gt.b], W=[sg.b])
                    kb.op("dve", lambda ec=ec, cix=cix: nc.vector.tensor_tensor(out=O[:, cix, :], in0=oo[:, ec, :], in1=sg[:], op=ALU.mult),
                          R=[oo.b, sg.b], W=[O.b])
            kb.dma("sp", g.Ymix[0:SEQW, t0:t0 + 128].rearrange("(c p) t -> p c t", p=128), O[:], R=[O.b], sbuf=O.b)
        kb.barrier()


def consts():
    c = {}
    c["c_ident"] = np.eye(128, dtype=np.float32)
    c["c_ones"] = np.ones((128, 128), np.float32)
    blk = np.zeros((128, 128), np.float32)
    blk[:64, :64] = 1
    blk[64:, 64:] = 1
    c["c_blk"] = blk
    sw = np.zeros((128, 128), np.float32)
    for i in range(64):
        sw[i, i + 64] = 1
        sw[i + 64, i] = 1
    c["c_swap"] = sw
    ys = np.zeros((128, 32, 64), np.float32)
    for k in range(128):
        for tl in range(32):
            ys[k, tl, (k // 64) * 32 + tl] = 1
    c["c_ysel"] = ys.reshape(128, 32 * 64)
    rot = np.zeros((128, 4), np.float32)
    inv = (1.0 / (np.float32(10000.0) ** (np.arange(0, 128, 2, dtype=np.float32) / np.float32(128)))).astype(np.float32)
    rot[:64, 0] = inv
    rot[64:, 0] = inv
    rot[:64, 1] = -1
    rot[64:, 1] = 1
    c["c_rot"] = rot
    Cc = 128
    lg = np.log1p(-np.exp2(-5.0 - np.arange(RET_H, dtype=np.float64)))
    idx = np.arange(Cc, dtype=np.float64)
    diff = idx[:, None] - idx[None, :]
    retT = np.zeros((128, RET_H, 128), np.float32)
    retq = np.zeros((128, RET_H, 128), np.float32)
    retk = np.zeros((128, RET_H), np.float32)
    retc = np.zeros((128, RET_H), np.float32)
    for h in range(RET_H):
        intra = np.where(diff >= 0, np.exp(np.maximum(diff, 0) * lg[h]), 0.0)
        retT[:, h, :] = intra.T
        retq[:, h, :] = np.exp((idx + 1.0) * lg[h])[None, :]
        retk[:, h] = np.exp((Cc - 1.0 - idx) * lg[h])
        retc[:, h] = np.exp(Cc * lg[h])
    c["c_retT"] = retT.reshape(128, -1)
    c["c_retq"] = retq.reshape(128, -1)
    c["c_retk"] = retk
    c["c_retc"] = retc
    return c


def prep_weights(inputs):
    return prep_weights_layers(inputs, [0, 1, 2, 3])


def prep_weights_layers(inputs, layers):
    w = {}
    w["mem_w_kv"] = np.ascontiguousarray(inputs["mem_w_kv"], dtype=np.float32)
    for li in layers:
        p = "l%d_" % li
        w[p + "w_in"] = inputs[p + "w_in"]
        w[p + "w_out"] = inputs[p + "w_out"]
        nr = np.asarray(inputs[p + "norms"], np.float32)
        w[p + "norms"] = np.concatenate([colvec(nr[i]) for i in range(4)], axis=1)
        if li % 2 == 0:
            muv = np.asarray(inputs[p + "mu"], np.float32)
            mus = np.zeros((128, 41), np.float32)
            mus[:, 0:36] = colvec(muv[0:4608])
            mus[0:96, 36] = muv[4608:4704]
            mus[0:96, 37] = muv[4704:4800]
            cgo = 4800
            if li == 2:
                mus[0:64, 38] = muv[4800:4864]
                cgo = 4864
            mus[:, 39] = muv[cgo:cgo + 128]
            mus[:, 40] = muv[cgo + 128:cgo + 256]
            w[p + "mu"] = mus
            for k in ("lora_w", "lora_a", "lora_g", "ffn_gate", "ffn_up", "ffn_down"):
                w[p + k] = inputs[p + k]
            if li == 2:
                w[p + "lora_v"] = inputs[p + "lora_v"]
            vv = np.asarray(inputs[p + "vecs"], np.float32)
            w[p + "vecs"] = np.concatenate([colvec(vv[i]) for i in range(vv.shape[0])], axis=1)
        else:
            gg = np.asarray(inputs[p + "gn"], np.float32)
            w[p + "gn"] = np.concatenate([colvec(gg[i]) for i in range(2)], axis=1)
            w[p + "router"] = inputs[p + "router"]
            w[p + "moe_gate"] = np.asarray(inputs[p + "moe_gate"]).reshape(NE * D, DFF)
            w[p + "moe_up"] = np.asarray(inputs[p + "moe_up"]).reshape(NE * D, DFF)
            w[p + "moe_down"] = np.asarray(inputs[p + "moe_down"]).reshape(NE * DFF, D)
    return w


def core_inputs(inputs, w, c, core, NB, S):
    x = np.asarray(inputs["x"])[core * NB:(core + 1) * NB, :S]
    m = {}
    m["xT"] = np.ascontiguousarray(x.reshape(NB * S, D).T)
    mem = np.asarray(inputs["mem"])[core * NB:(core + 1) * NB]
    m["memT"] = np.ascontiguousarray(mem.reshape(NB * MEMLEN, D).T)
    m["pos"] = np.ascontiguousarray(np.asarray(inputs["positions"])[core * NB:(core + 1) * NB, :S].reshape(1, NB * S)).astype(np.int32)
    m.update(w)
    m.update(c)
    return m


_CACHE = {}


def kernel(**inputs):
    NB, S, NCORE = 2, 2048, 8
    if "nc" not in _CACHE:
        _CACHE["nc"] = build(NB, S, [0, 1, 2, 3]).nc
    nc = _CACHE["nc"]
    w = prep_weights(inputs)
    c = consts()
    in_maps = [core_inputs(inputs, w, c, core, NB, S) for core in range(NCORE)]
    res = run_bass_kernel_spmd(nc, in_maps, core_ids=list(range(NCORE)))
    out = np.empty((NCORE * NB, S, D), np.float32)
    for core in range(NCORE):
        oT = res.results[core]["outT"]
        out[core * NB:(core + 1) * NB] = oT.T.reshape(NB, S, D)
    return out
```

```python
import numpy as np
from contextlib import ExitStack
import concourse.bass as bass
import concourse.mybir as mybir
from concourse.bass_utils import run_bass_kernel_spmd

F32 = mybir.dt.float32
BF16 = mybir.dt.bfloat16
I32 = mybir.dt.int32
AF = mybir.ActivationFunctionType
ALU = mybir.AluOpType
AX = mybir.AxisListType

D = 2048
MEMLEN = 256
SEQW = 1536
MEMW = 512
NH = 24
DFF = 5632
NE = 8
ALPHA = float(8.0 ** 0.25)
LN_EPS = 1e-5
GN_EPS_RWKV = 1e-5 * 64
RET_H = 6
RET_DK = 128
RET_DV = 256
TWO_PI = float(2 * np.pi)


EMBED = True


class Buf:
    __slots__ = ("name", "lw", "rd", "dsem", "persist")

    def __init__(self, name, persist=False):
        self.name = name
        self.lw = None
        self.rd = []
        self.dsem = None
        self.persist = persist


class KB:
    def __init__(self, nc):
        self.nc = nc
        self.es = ExitStack()
        self.eng = {"pe": nc.tensor, "act": nc.scalar, "dve": nc.vector, "pool": nc.gpsimd, "sp": nc.sync}
        self.sems = {}
        self.cnt = {}
        self.waited = {e: {} for e in self.eng}
        for e in self.eng:
            self._newsem("E" + e)
        self.free_dsems = []
        self.ndsem = 0
        self.bufs = []
        self.uid = 0
        self.ninstr = 0

    def _newsem(self, name):
        self.sems[name] = self.es.enter_context(self.nc.semaphore(name))
        self.cnt[name] = 0

    def get_dsem(self):
        if self.free_dsems:
            return self.free_dsems.pop()
        name = "D%d" % self.ndsem
        self.ndsem += 1
        self._newsem(name)
        return name

    def buf(self, name, persist=False):
        b = Buf(name, persist)
        self.bufs.append(b)
        return b

    def name(self, base):
        self.uid += 1
        return "%s_%d" % (base, self.uid)

    def _wait(self, e, toks, embed=False):
        best = {}
        for t in toks:
            if t is None:
                continue
            s, v = t
            if s[0] == "D":
                v = self.cnt[s]
            if best.get(s, 0) < v:
                best[s] = v
        need = []
        for s, v in best.items():
            if s == "E" + e and e in ("pe", "sp"):
                continue
            if self.waited[e].get(s, 0) >= v:
                continue
            need.append((s, v))
            self.waited[e][s] = v
        last = None
        if embed and need:
            last = need.pop()
        for s, v in need:
            self.eng[e].wait_ge(self.sems[s], v)
            self.ninstr += 1
        return last

    def _deps(self, R, W, group):
        toks = [b.lw for b in R]
        for b in W:
            if not (group and b.lw is not None and b.lw[0][0] == "D"):
                toks.append(b.lw)
            toks.extend(b.rd)
        return toks

    def _update(self, tok, R, W):
        for b in R:
            b.rd = [t for t in b.rd if t[0] != tok[0]] + [tok]
        for b in W:
            b.lw = tok
            b.rd = []

    def op(self, e, fn, R=(), W=()):
        last = self._wait(e, self._deps(R, W, False), EMBED)
        ins = fn()
        if last is not None:
            ins._wait_ge(self.sems[last[0]], last[1])
        s = "E" + e
        self.cnt[s] += 1
        ins.then_inc(self.sems[s], 1)
        self.ninstr += 1
        tok = (s, self.cnt[s])
        self._update(tok, R, W)
        return tok

    def dma(self, q, out, in_, R=(), W=(), group=True, sbuf=None):
        last = self._wait(q, self._deps(R, W, group), EMBED)
        ins = self.eng[q].dma_start(out=out, in_=in_)
        if last is not None:
            ins._wait_ge(self.sems[last[0]], last[1])
        if sbuf.dsem is None:
            sbuf.dsem = self.get_dsem()
        s = sbuf.dsem
        self.cnt[s] += 16
        ins.then_inc(self.sems[s], 16)
        self.ninstr += 1
        tok = (s, self.cnt[s])
        self._update(tok, R, W)
        return tok

    def barrier(self):
        for e in self.eng:
            for s in self.sems:
                v = self.cnt[s]
                if v > self.waited[e].get(s, 0):
                    self.eng[e].wait_ge(self.sems[s], v)
                    self.ninstr += 1
                    self.waited[e][s] = v
        keep = []
        for b in self.bufs:
            b.lw = None
            b.rd = []
            if b.dsem is not None:
                self.free_dsems.append(b.dsem)
                b.dsem = None
            if b.persist:
                keep.append(b)
        self.bufs = keep


class Tile:
    def __init__(self, kb, st, name, shape, dtype, persist=False):
        self.t = st.enter_context(kb.nc.sbuf_tensor(kb.name(name), list(shape), dtype))
        self.b = kb.buf(name, persist)
        self.shape = shape

    def __getitem__(self, k):
        return self.t[k]


def colvec(v, n=None):
    v = np.asarray(v, np.float32).reshape(-1)
    nn = v.shape[0]
    nch = (nn + 127) // 128
    o = np.zeros((nch * 128,), np.float32)
    o[:nn] = v
    return np.ascontiguousarray(o.reshape(nch, 128).T)


class G:
    pass


def build(NB, S, layers, stop_after=None, dbg=None, ext_in=(), skip=()):
    T = NB * S
    nc = bass.Bass("TRN2", target_bir_lowering=False)
    kb = KB(nc)
    g = G()
    g.nc, g.kb, g.NB, g.S, g.T = nc, kb, NB, S, T
    g.din = {}

    def inp(name, shape, dt=F32):
        h = nc.dram_tensor(name, list(shape), dt, kind="ExternalInput")
        g.din[name] = h
        return h

    def scratch(name, shape, dt=F32):
        if name in ext_in:
            return nc.dram_tensor(name, list(shape), dt, kind="ExternalInput")
        return nc.dram_tensor(name, list(shape), dt)
    g.skip = skip

    inp("xT", [D, T])
    inp("memT", [D, NB * MEMLEN])
    inp("pos", [1, T], I32)
    inp("mem_w_kv", [D, 2 * MEMW])
    inp("c_ident", [128, 128])
    inp("c_ones", [128, 128])
    inp("c_blk", [128, 128])
    inp("c_swap", [128, 128])
    inp("c_ysel", [128, 32 * 64])
    inp("c_rot", [128, 4])
    inp("c_retT", [128, RET_H * 128])
    inp("c_retq", [128, RET_H * 128])
    inp("c_retk", [128, RET_H])
    inp("c_retc", [128, RET_H])
    LW = []
    for li in range(4):
        p = "l%d_" % li
        L = G()
        LW.append(L)
        if li not in layers:
            continue
        L.rwkv = (li % 2 == 0)
        _ne = 1 if L.rwkv else NE
        L.wtG = scratch(p + "wtG", [_ne * 44, 128, 16 * 128], BF16)
        L.wtU = scratch(p + "wtU", [_ne * 44, 128, 16 * 128], BF16)
        L.wtD = scratch(p + "wtD", [_ne * 16, 128, 44 * 128], BF16)
        L.withv = (li == 2)
        L.C = 5568 if li == 0 else (5632 if li == 2 else 5120)
        L.w_in = inp(p + "w_in", [D, L.C])
        L.w_out = inp(p + "w_out", [D, D])
        L.norms = inp(p + "norms", [128, 4 * 16])
        if L.rwkv:
            L.nmix = L.C - MEMW
            L.mu = inp(p + "mu", [128, 41])
            L.lora_w = inp(p + "lora_w", [96, SEQW])
            L.lora_a = inp(p + "lora_a", [96, SEQW])
            if L.withv:
                L.lora_v = inp(p + "lora_v", [64, SEQW])
            L.lora_g = inp(p + "lora_g", [256, SEQW])
            L.vecs = inp(p + "vecs", [128, (8 if L.withv else 7) * 12])
            L.ffn_gate = inp(p + "ffn_gate", [D, DFF])
            L.ffn_up = inp(p + "ffn_up", [D, DFF])
            L.ffn_down = inp(p + "ffn_down", [DFF, D])
        else:
            L.gn = inp(p + "gn", [128, 2 * 12])
            L.router = inp(p + "router", [D, NE])
            L.ffn_gate = inp(p + "moe_gate", [NE * D, DFF])
            L.ffn_up = inp(p + "moe_up", [NE * D, DFF])
            L.ffn_down = inp(p + "moe_down", [NE * DFF, D])
    g.out = nc.dram_tensor("outT", [D, T], F32, kind="ExternalOutput")

    g.xres = [scratch("xresA", [D, T]), scratch("xresB", [D, T])]
    g.Pd = scratch("Pd", [5632, T])
    g.Zd = scratch("Zd", [D, T])
    g.Ymix = scratch("Ymix", [D, T], BF16)
    g.Q = [scratch("Q%d" % i, [NB, SEQW, S]) for i in range(5)]
    for i in range(5):
        setattr(g, "Q%d" % i, g.Q[i])
    g.Vtok = scratch("Vtok", [NB, S, SEQW])
    g.Vfirst = scratch("Vfirst", [SEQW, T])
    g.Bonus = scratch("Bonus", [SEQW, T])
    g.Gd = scratch("Gd", [SEQW, T])
    g.Ytok = scratch("Ytok", [NB, S, SEQW])
    g.CombT = scratch("CombT", [NE, T])

    pst = kb.es
    g.ident = Tile(kb, pst, "ident", [128, 128], F32, True)
    g.identb = Tile(kb, pst, "identb", [128, 128], BF16, True)
    g.ones = Tile(kb, pst, "ones", [128, 128], F32, True)
    g.blk = Tile(kb, pst, "blk", [128, 128], F32, True)
    g.swap = Tile(kb, pst, "swap", [128, 128], F32, True)
    g.memk = Tile(kb, pst, "memk", [128, 4, NB, MEMLEN], BF16, True)
    g.memv = Tile(kb, pst, "memv", [128, NB, 2, MEMW], BF16, True)
    g.ps = pst.enter_context(nc.psum_tensor("psum_all", [128, 8, 512], F32))
    g.pb = [kb.buf("psb%d" % i, True) for i in range(8)]
    g.pbi = 0

    for name, t in (("c_ident", g.ident), ("c_ones", g.ones), ("c_blk", g.blk), ("c_swap", g.swap)):
        kb.dma("sp", t[:], g.din[name][:, :], W=[t.b], sbuf=t.b)
    kb.dma("pool", g.identb[:], g.din["c_ident"][:, :], W=[g.identb.b], sbuf=g.identb.b)
    kb.barrier()

    stage_setup_mem(g)
    if stop_after == "mem":
        return finish(g, dbg)
    cur = 0
    xin = g.din["xT"]
    for li in layers:
        L = LW[li]
        xo1 = g.xres[0]
        xo2 = g.xres[1]
        if "proj" not in g.skip:
            stage_proj(g, L, xin)
        if stop_after == "proj%d" % li:
            return finish(g, dbg)
        if "mix" in g.skip:
            pass
        elif L.rwkv:
            stage_rwkv_prep(g, L)
            if stop_after == "prep%d" % li:
                return finish(g, dbg)
            stage_rwkv_scan(g, L)
            if stop_after == "scan%d" % li:
                return finish(g, dbg)
            stage_rwkv_post(g, L)
        else:
            stage_retention(g, L)
        if stop_after == "mix%d" % li:
            return finish(g, dbg)
        if "memattn" not in g.skip:
            stage_memattn(g, L)
        if stop_after == "mem%d" % li:
            return finish(g, dbg)
        stage_wout(g, L, xin)
        stage_ln(g, L, 0, xo1)
        if stop_after == "ln1_%d" % li:
            return finish(g, dbg)
        if not L.rwkv:
            stage_router(g, L, xo1)
            if stop_after == "router%d" % li:
                return finish(g, dbg)
        stage_ffn(g, L, xo1)
        last = (li == layers[-1])
        stage_ln(g, L, 2, g.out if last else xo2)
        if stop_after == "ln2_%d" % li:
            return finish(g, dbg)
        xin = xo2
    return finish(g, dbg)


def finish(g, dbg):
    kb, nc = g.kb, g.nc
    kb.barrier()
    for name in (dbg or ()):
        src = getattr(g, name)
        o = nc.dram_tensor("dbg_" + name, list(src.shape), src.dtype, kind="ExternalOutput")
        db = kb.buf("dbg")
        if len(src.shape) == 2:
            kb.dma("sp", o[:, :], src[:, :], sbuf=db)
        else:
            kb.dma("sp", o[:, :, :], src[:, :, :], sbuf=db)
    kb.barrier()
    g.kb.es.close()
    return g


def psum_bank(g):
    i = g.pbi
    g.pbi = (g.pbi + 1) % 8
    return g.ps[:, i, :], g.pb[i]


def wtile_ap(W, r0, K, c0, ncol):
    return W[r0:r0 + K, c0:c0 + ncol].rearrange("(kc p) n -> p kc n", p=128)


def act_T_ap(A, r0, K, t0, nt):
    return A[r0:r0 + K, t0:t0 + nt].rearrange("(kc p) t -> p kc t", p=128)


def stage_setup_mem(g):
    kb, nc = g.kb, g.nc
    NB = g.NB
    with ExitStack() as st:
        memT = Tile(kb, st, "memT", [128, 16, NB * MEMLEN], BF16)
        wkv = Tile(kb, st, "wkv", [128, 16, 2 * MEMW], BF16)
        kb.dma("pool", memT[:], act_T_ap(g.din["memT"], 0, D, 0, NB * MEMLEN), W=[memT.b], sbuf=memT.b)
        kb.dma("pool", wkv[:], wtile_ap(g.din["mem_w_kv"], 0, D, 0, 2 * MEMW), W=[wkv.b], sbuf=wkv.b)
        for h in range(4):
            ps, pb = psum_bank(g)
            for kc in range(16):
                kb.op("pe", lambda kc=kc: nc.tensor.matmul(ps[:, 0:NB * MEMLEN], wkv[:, kc, h * 128:(h + 1) * 128],
                                                           memT[:, kc, :], start=(kc == 0), stop=(kc == 15)),
                      R=[wkv.b, memT.b], W=[pb])
            kb.op("act", lambda: nc.scalar.copy(out=g.memk[:, h, :, :].rearrange("p b m -> p (b m)"), in_=ps[:, 0:NB * MEMLEN]),
                  R=[pb], W=[g.memk.b])
        for b in range(NB):
            for mc in range(2):
                ps, pb = psum_bank(g)
                c0 = b * MEMLEN + mc * 128
                for kc in range(16):
                    kb.op("pe", lambda kc=kc: nc.tensor.matmul(ps[:, 0:MEMW], memT[:, kc, c0:c0 + 128],
                                                               wkv[:, kc, MEMW:2 * MEMW], start=(kc == 0), stop=(kc == 15)),
                          R=[wkv.b, memT.b], W=[pb])
                kb.op("act", lambda: nc.scalar.copy(out=g.memv[:, b, mc, :], in_=ps[:, 0:MEMW]), R=[pb], W=[g.memv.b])
        kb.barrier()


def stage_proj(g, L, xin):
    kb, nc, T = g.kb, g.nc, g.T
    TSB = min(T, 2048)
    with ExitStack() as st:
        xT = Tile(kb, st, "xT", [128, 16, TSB], BF16)
        wt = [Tile(kb, st, "wt", [128, 16, 256], BF16) for _ in range(2)]
        ot = [Tile(kb, st, "ot", [128, TSB], F32) for _ in range(2)]
        wi = 0
        oi = 0
        for t0 in range(0, T, TSB):
            kb.dma("pool", xT[:], act_T_ap(xin, 0, D, t0, TSB), W=[xT.b], sbuf=xT.b)
            for c0 in range(0, L.C, 256):
                ncol = min(256, L.C - c0)
                w = wt[wi]
                wi ^= 1
                kb.dma("pool", w[:, :, 0:ncol], wtile_ap(L.w_in, 0, D, c0, ncol), W=[w.b], sbuf=w.b)
                for cc in range(0, ncol, 128):
                    m = min(128, ncol - cc)
                    o = ot[oi]
                    oi ^= 1
                    for tb in range(0, TSB, 512):
                        ps, pb = psum_bank(g)
                        for kc in range(16):
                            kb.op("pe", lambda kc=kc: nc.tensor.matmul(ps[0:m, :], w[:, kc, cc:cc + m], xT[:, kc, tb:tb + 512],
                                                                       start=(kc == 0), stop=(kc == 15)),
                                  R=[w.b, xT.b], W=[pb])
                        kb.op("act", lambda: nc.scalar.copy(out=o[0:m, tb:tb + 512], in_=ps[0:m, :]), R=[pb], W=[o.b])
                    kb.dma("sp", g.Pd[c0 + cc:c0 + cc + m, t0:t0 + TSB], o[0:m, :], R=[o.b], sbuf=o.b)
        kb.barrier()


def stage_wout(g, L, xin):
    kb, nc, T = g.kb, g.nc, g.T
    TSB = min(T, 2048)
    with ExitStack() as st:
        yT = Tile(kb, st, "yT", [128, 16, TSB], BF16)
        wt = [Tile(kb, st, "wt", [128, 16, 256], BF16) for _ in range(2)]
        xt = [Tile(kb, st, "xt", [128, TSB], F32) for _ in range(2)]
        wi = 0
        oi = 0
        for t0 in range(0, T, TSB):
            kb.dma("sp", yT[:], act_T_ap(g.Ymix, 0, D, t0, TSB), W=[yT.b], sbuf=yT.b)
            for c0 in range(0, D, 256):
                w = wt[wi]
                wi ^= 1
                kb.dma("pool", w[:], wtile_ap(L.w_out, 0, D, c0, 256), W=[w.b], sbuf=w.b)
                for cc in range(0, 256, 128):
                    o = xt[oi]
                    oi ^= 1
                    kb.dma("sp", o[:], xin[c0 + cc:c0 + cc + 128, t0:t0 + TSB], W=[o.b], sbuf=o.b)
                    for tb in range(0, TSB, 512):
                        ps, pb = psum_bank(g)
                        for kc in range(16):
                            kb.op("pe", lambda kc=kc: nc.tensor.matmul(ps, w[:, kc, cc:cc + 128], yT[:, kc, tb:tb + 512],
                                                                       start=(kc == 0), stop=(kc == 15)),
                                  R=[w.b, yT.b], W=[pb])
                        kb.op("dve", lambda: nc.vector.scalar_tensor_tensor(out=o[:, tb:tb + 512], in0=o[:, tb:tb + 512], scalar=ALPHA,
                                                                            in1=ps, op0=ALU.mult, op1=ALU.add),
                              R=[pb, o.b], W=[o.b])
                    kb.dma("sp", g.Zd[c0 + cc:c0 + cc + 128, t0:t0 + TSB], o[:], R=[o.b], sbuf=o.b)
        kb.barrier()


def stage_ln(g, L, which, dst):
    kb, nc, T = g.kb, g.nc, g.T
    TB = 512
    with ExitStack() as st:
        nrm = Tile(kb, st, "nrm", [128, 64], F32)
        kb.dma("sp", nrm[:], L.norms[:, :], W=[nrm.b], sbuf=nrm.b)
        zt = [Tile(kb, st, "z", [128, 16, TB], F32) for _ in range(2)]
        sq = Tile(kb, st, "sq", [128, 16, TB], F32)
        mean = Tile(kb, st, "mean", [128, TB], F32)
        rstd = Tile(kb, st, "rstd", [128, TB], F32)
        zi = 0
        for t0 in range(0, T, TB):
            z = zt[zi]
            zi ^= 1
            kb.dma("sp", z[:], act_T_ap(g.Zd, 0, D, t0, TB), W=[z.b], sbuf=z.b)
            ps, pb = psum_bank(g)
            for kc in range(16):
                kb.op("pe", lambda kc=kc: nc.tensor.matmul(ps, g.ones[:], z[:, kc, :], start=(kc == 0), stop=(kc == 15)),
                      R=[g.ones.b, z.b], W=[pb])
            kb.op("act", lambda: nc.scalar.mul(out=mean[:], in_=ps, mul=1.0 / D), R=[pb], W=[mean.b])
            kb.op("dve", lambda: nc.vector.tensor_tensor(out=z[:], in0=z[:], in1=mean[:].unsqueeze(1).to_broadcast([128, 16, TB]),
                                                         op=ALU.subtract), R=[z.b, mean.b], W=[z.b])
            kb.op("act", lambda: nc.scalar.activation(out=sq[:], in_=z[:], func=AF.Square), R=[z.b], W=[sq.b])
            ps2, pb2 = psum_bank(g)
            for kc in range(16):
                kb.op("pe", lambda kc=kc: nc.tensor.matmul(ps2, g.ones[:], sq[:, kc, :], start=(kc == 0), stop=(kc == 15)),
                      R=[g.ones.b, sq.b], W=[pb2])
            kb.op("dve", lambda: nc.vector.tensor_scalar(out=rstd[:], in0=ps2, scalar1=1.0 / D, scalar2=LN_EPS,
                                                         op0=ALU.mult, op1=ALU.add), R=[pb2], W=[rstd.b])
            kb.op("act", lambda: nc.scalar.sqrt(out=rstd[:], in_=rstd[:]), R=[rstd.b], W=[rstd.b])
            kb.op("dve", lambda: nc.vector.reciprocal(out=rstd[:], in_=rstd[:]), R=[rstd.b], W=[rstd.b])
            kb.op("dve", lambda: nc.vector.tensor_tensor(out=z[:], in0=z[:], in1=rstd[:].unsqueeze(1).to_broadcast([128, 16, TB]),
                                                         op=ALU.mult), R=[z.b, rstd.b], W=[z.b])
            for kc in range(16):
                kb.op("act", lambda kc=kc: nc.scalar.activation(out=z[:, kc, :], in_=z[:, kc, :], func=AF.Identity,
                                                                scale=nrm[:, which * 16 + kc:which * 16 + kc + 1],
                                                                bias=nrm[:, (which + 1) * 16 + kc:(which + 1) * 16 + kc + 1]),
                      R=[z.b, nrm.b], W=[z.b])
            kb.dma("sp", act_T_ap(dst, 0, D, t0, TB), z[:], R=[z.b], sbuf=z.b)
        kb.barrier()


def stage_router(g, L, xsrc):
    kb, nc, T = g.kb, g.nc, g.T
    with ExitStack() as st:
        rw = Tile(kb, st, "rw", [128, 16, NE], F32)
        kb.dma("sp", rw[:], wtile_ap(L.router, 0, D, 0, NE), W=[rw.b], sbuf=rw.b)
        xt = [Tile(kb, st, "xr", [128, 16, 128], F32) for _ in range(2)]
        lg = Tile(kb, st, "lg", [128, NE], F32)
        m1 = Tile(kb, st, "m1", [128, 1], F32)
        m2 = Tile(kb, st, "m2", [128, 1], F32)
        k1 = Tile(kb, st, "k1", [128, NE], F32)
        k2 = Tile(kb, st, "k2", [128, NE], F32)
        l2 = Tile(kb, st, "l2", [128, NE], F32)
        w1 = Tile(kb, st, "w1", [128, 1], F32)
        w2 = Tile(kb, st, "w2", [128, 1], F32)
        cb = Tile(kb, st, "cb", [128, NE], F32)
        cT = [Tile(kb, st, "cT", [NE, 128], F32) for _ in range(2)]
        xi = 0
        for t0 in range(0, T, 128):
            x = xt[xi]
            c = cT[xi]
            xi ^= 1
            kb.dma("sp", x[:], act_T_ap(xsrc, 0, D, t0, 128), W=[x.b], sbuf=x.b)
            ps, pb = psum_bank(g)
            for kc in range(16):
                kb.op("pe", lambda kc=kc: nc.tensor.matmul(ps[:, 0:NE], x[:, kc, :], rw[:, kc, :], start=(kc == 0), stop=(kc == 15)),
                      R=[x.b, rw.b], W=[pb])
            kb.op("act", lambda: nc.scalar.copy(out=lg[:], in_=ps[:, 0:NE]), R=[pb], W=[lg.b])
            kb.op("dve", lambda: nc.vector.reduce_max(out=m1[:], in_=lg[:], axis=AX.X), R=[lg.b], W=[m1.b])
            kb.op("dve", lambda: nc.vector.tensor_scalar(out=k1[:], in0=lg[:], scalar1=m1[:, 0:1], scalar2=None, op0=ALU.is_equal),
                  R=[lg.b, m1.b], W=[k1.b])
            kb.op("dve", lambda: nc.vector.scalar_tensor_tensor(out=l2[:], in0=k1[:], scalar=-1e30, in1=lg[:], op0=ALU.mult, op1=ALU.add),
                  R=[k1.b, lg.b], W=[l2.b])
            kb.op("dve", lambda: nc.vector.reduce_max(out=m2[:], in_=l2[:], axis=AX.X), R=[l2.b], W=[m2.b])
            kb.op("dve", lambda: nc.vector.tensor_scalar(out=k2[:], in0=l2[:], scalar1=m2[:, 0:1], scalar2=None, op0=ALU.is_equal),
                  R=[l2.b, m2.b], W=[k2.b])
            kb.op("dve", lambda: nc.vector.tensor_tensor(out=w2[:], in0=m2[:], in1=m1[:], op=ALU.subtract), R=[m1.b, m2.b], W=[w2.b])
            kb.op("act", lambda: nc.scalar.activation(out=w2[:], in_=w2[:], func=AF.Sigmoid), R=[w2.b], W=[w2.b])
            kb.op("dve", lambda: nc.vector.tensor_scalar(out=w1[:], in0=w2[:], scalar1=-1.0, scalar2=1.0, op0=ALU.mult, op1=ALU.add),
                  R=[w2.b], W=[w1.b])
            kb.op("dve", lambda: nc.vector.tensor_scalar(out=cb[:], in0=k1[:], scalar1=w1[:, 0:1], scalar2=None, op0=ALU.mult),
                  R=[k1.b, w1.b], W=[cb.b])
            kb.op("dve", lambda: nc.vector.scalar_tensor_tensor(out=cb[:], in0=k2[:], scalar=w2[:, 0:1], in1=cb[:], op0=ALU.mult, op1=ALU.add),
                  R=[k2.b, w2.b, cb.b], W=[cb.b])
            ps2, pb2 = psum_bank(g)
            kb.op("pe", lambda: nc.tensor.transpose(ps2[0:NE, 0:128], cb[:], g.ident[:]), R=[cb.b, g.ident.b], W=[pb2])
            kb.op("act", lambda: nc.scalar.copy(out=c[:], in_=ps2[0:NE, 0:128]), R=[pb2], W=[c.b])
            kb.dma("sp", g.CombT[:, t0:t0 + 128], c[:], R=[c.b], sbuf=c.b)
        kb.barrier()


def stage_ffn(g, L, xsrc):
    kb, nc, T = g.kb, g.nc, g.T
    moe = not L.rwkv
    ne = NE if moe else 1
    TB = 512
    NF = DFF // 128
    with ExitStack() as st:
        xT = Tile(kb, st, "xT", [128, 16, TB], BF16)
        hT = Tile(kb, st, "hT", [128, NF, TB], BF16)
        NWB = 4
        NWD = 3
        wg = [Tile(kb, st, "wg", [128, 16, 128], BF16) for _ in range(NWB)]
        wu = [Tile(kb, st, "wu", [128, 16, 128], BF16) for _ in range(NWB)]
        wd = [Tile(kb, st, "wd", [128, NF, 128], BF16) for _ in range(NWD)]
        sg = [Tile(kb, st, "sg", [128, TB], F32) for _ in range(2)]
        xr = [Tile(kb, st, "xr", [128, TB], F32) for _ in range(2)]
        if moe:
            facc = Tile(kb, st, "facc", [128, 16, TB], F32)
            cbc = Tile(kb, st, "cbc", [128, NE, TB], F32)
        i2 = 0
        id_ = 0
        dG, dU, dD = {}, {}, {}
        for t0 in range(0, T, TB):
            kb.dma("pool", xT[:], act_T_ap(xsrc, 0, D, t0, TB), W=[xT.b], sbuf=xT.b)
            if moe:
                for e in range(NE):
                    kb.dma("sp", cbc[:, e, :], g.CombT[e:e + 1, t0:t0 + TB].partition_broadcast(128), W=[cbc.b], sbuf=cbc.b)
            for e in range(ne):
                for f in range(NF):
                    nwb = 2 if t0 == 0 else NWB
                    a = wg[i2 % nwb]
                    b_ = wu[i2 % nwb]
                    s_ = sg[i2 % 2]
                    i2 += 1
                    ix = e * NF + f
                    cG = L.wtG[ix, :, :].rearrange("p (kc n) -> p kc n", n=128)
                    cU = L.wtU[ix, :, :].rearrange("p (kc n) -> p kc n", n=128)
                    if t0 == 0:
                        dG[ix] = kb.buf("dG")
                        dU[ix] = kb.buf("dU")
                        kb.dma("pool", a[:], wtile_ap(L.ffn_gate, e * D, D, f * 128, 128), W=[a.b], sbuf=a.b)
                        kb.dma("pool", b_[:], wtile_ap(L.ffn_up, e * D, D, f * 128, 128), W=[b_.b], sbuf=b_.b)
                        if T > TB:
                            kb.dma("sp", cG, a[:], R=[a.b], W=[dG[ix]], sbuf=a.b)
                            kb.dma("sp", cU, b_[:], R=[b_.b], W=[dU[ix]], sbuf=b_.b)
                    else:
                        kb.dma("sp", a[:], cG, R=[dG[ix]], W=[a.b], sbuf=a.b)
                        kb.dma("sp", b_[:], cU, R=[dU[ix]], W=[b_.b], sbuf=b_.b)
                    psg, pbg = psum_bank(g)
                    for kc in range(16):
                        kb.op("pe", lambda kc=kc: nc.tensor.matmul(psg, a[:, kc, :], xT[:, kc, :], start=(kc == 0), stop=(kc == 15)),
                              R=[a.b, xT.b], W=[pbg])
                    psu, pbu = psum_bank(g)
                    for kc in range(16):
                        kb.op("pe", lambda kc=kc: nc.tensor.matmul(psu, b_[:, kc, :], xT[:, kc, :], start=(kc == 0), stop=(kc == 15)),
                              R=[b_.b, xT.b], W=[pbu])
                    kb.op("act", lambda: nc.scalar.activation(out=s_[:], in_=psg, func=AF.Silu), R=[pbg], W=[s_.b])
                    if moe:
                        kb.op("pool", lambda: nc.gpsimd.tensor_tensor(out=s_[:], in0=s_[:], in1=cbc[:, e, :], op=ALU.mult),
                              R=[s_.b, cbc.b], W=[s_.b])
                    kb.op("dve", lambda: nc.vector.tensor_tensor(out=hT[:, f, :], in0=s_[:], in1=psu, op=ALU.mult),
                          R=[s_.b, pbu], W=[hT.b])
                for dc in range(16):
                    w = wd[id_ % (2 if t0 == 0 else NWD)]
                    x_ = xr[id_ % 2]
                    id_ += 1
                    ixd = e * 16 + dc
                    cD = L.wtD[ixd, :, :].rearrange("p (kc n) -> p kc n", n=128)
                    if t0 == 0:
                        dD[ixd] = kb.buf("dD")
                        kb.dma("pool", w[:], wtile_ap(L.ffn_down, e * DFF, DFF, dc * 128, 128), W=[w.b], sbuf=w.b)
                        if T > TB:
                            kb.dma("sp", cD, w[:], R=[w.b], W=[dD[ixd]], sbuf=w.b)
                    else:
                        kb.dma("sp", w[:], cD, R=[dD[ixd]], W=[w.b], sbuf=w.b)
                    lastE = (e == ne - 1)
                    if lastE:
                        kb.dma("sp", x_[:], xsrc[dc * 128:(dc + 1) * 128, t0:t0 + TB], W=[x_.b], sbuf=x_.b)
                    ps, pb = psum_bank(g)
                    for f in range(NF):
                        kb.op("pe", lambda f=f: nc.tensor.matmul(ps, w[:, f, :], hT[:, f, :], start=(f == 0), stop=(f == NF - 1)),
                              R=[w.b, hT.b], W=[pb])
                    if moe:
                        if e == 0:
                            kb.op("act", lambda: nc.scalar.copy(out=facc[:, dc, :], in_=ps), R=[pb], W=[facc.b])
                        else:
                            kb.op("dve", lambda: nc.vector.tensor_tensor(out=facc[:, dc, :], in0=facc[:, dc, :], in1=ps, op=ALU.add),
                                  R=[pb, facc.b], W=[facc.b])
                        if lastE:
                            kb.op("dve", lambda: nc.vector.scalar_tensor_tensor(out=x_[:], in0=x_[:], scalar=ALPHA, in1=facc[:, dc, :],
                                                                                op0=ALU.mult, op1=ALU.add),
                                  R=[x_.b, facc.b], W=[x_.b])
                    else:
                        kb.op("dve", lambda: nc.vector.scalar_tensor_tensor(out=x_[:], in0=x_[:], scalar=ALPHA, in1=ps,
                                                                            op0=ALU.mult, op1=ALU.add),
                              R=[x_.b, pb], W=[x_.b])
                    if lastE:
                        kb.dma("sp", g.Zd[dc * 128:(dc + 1) * 128, t0:t0 + TB], x_[:], R=[x_.b], sbuf=x_.b)
        kb.barrier()


def stage_memattn(g, L):
    kb, nc, T, S = g.kb, g.nc, g.T, g.S
    q0 = L.C - MEMW
    scale = float(128 ** -0.5)
    with ExitStack() as st:
        qt = [Tile(kb, st, "q", [128, 4, 128], BF16) for _ in range(2)]
        ot = [Tile(kb, st, "o", [128, 4, 128], BF16) for _ in range(2)]
        mx = Tile(kb, st, "mx", [128, 1], F32)
        rs = Tile(kb, st, "rs", [128, 1], F32)
        ee = Tile(kb, st, "ee", [128, MEMLEN], F32)
        pp = Tile(kb, st, "pp", [128, MEMLEN], BF16)
        pT = Tile(kb, st, "pT", [128, 2, 128], BF16)
        pst_ = st.enter_context(nc.psum_tensor(kb.name("pstb"), [128, 2, 128], BF16)) if False else None
        qi = 0
        for t0 in range(0, T, 128):
            b = t0 // S
            q = qt[qi]
            o = ot[qi]
            qi ^= 1
            kb.dma("pool", q[:], g.Pd[q0:q0 + MEMW, t0:t0 + 128].rearrange("(h p) t -> p h t", p=128), W=[q.b], sbuf=q.b)
            for h in range(4):
                ps, pb = psum_bank(g)
                kb.op("pe", lambda: nc.tensor.matmul(ps[:, 0:MEMLEN], q[:, h, :], g.memk[:, h, b, :], start=True, stop=True),
                      R=[q.b, g.memk.b], W=[pb])
                kb.op("dve", lambda: nc.vector.reduce_max(out=mx[:], in_=ps[:, 0:MEMLEN], axis=AX.X), R=[pb], W=[mx.b])
                kb.op("dve", lambda: nc.vector.tensor_scalar(out=mx[:], in0=mx[:], scalar1=-scale, scalar2=None, op0=ALU.mult),
                      R=[mx.b], W=[mx.b])
                kb.op("act", lambda: nc.scalar.activation(out=ee[:], in_=ps[:, 0:MEMLEN], func=AF.Exp, bias=mx[:, 0:1], scale=scale,
                                                          accum_out=rs[:, 0:1]), R=[pb, mx.b], W=[ee.b, rs.b])
                kb.op("dve", lambda: nc.vector.reciprocal(out=rs[:], in_=rs[:]), R=[rs.b], W=[rs.b])
                kb.op("dve", lambda: nc.vector.tensor_scalar(out=pp[:], in0=ee[:], scalar1=rs[:, 0:1], scalar2=None, op0=ALU.mult),
                      R=[ee.b, rs.b], W=[pp.b])
                for mc in range(2):
                    ps2, pb2 = psum_bank(g)
                    ps2b = ps2.bitcast(BF16)
                    kb.op("pe", lambda: nc.tensor.transpose(ps2b[:, 0:128], pp[:, mc * 128:(mc + 1) * 128], g.identb[:]),
                          R=[pp.b, g.identb.b], W=[pb2])
                    kb.op("act", lambda: nc.scalar.copy(out=pT[:, mc, :], in_=ps2b[:, 0:128]), R=[pb2], W=[pT.b])
                ps3, pb3 = psum_bank(g)
                for mc in range(2):
                    kb.op("pe", lambda mc=mc: nc.tensor.matmul(ps3[:, 0:128], g.memv[:, b, mc, h * 128:(h + 1) * 128], pT[:, mc, :],
                                                               start=(mc == 0), stop=(mc == 1)),
                          R=[g.memv.b, pT.b], W=[pb3])
                kb.op("act", lambda: nc.scalar.copy(out=o[:, h, :], in_=ps3[:, 0:128]), R=[pb3], W=[o.b])
            kb.dma("sp", g.Ymix[SEQW:D, t0:t0 + 128].rearrange("(h p) t -> p h t", p=128), o[:], R=[o.b], sbuf=o.b)
        kb.barrier()


def _mix_tile(g, st_tiles, pt, m, mu_ap, TBp, out_ap, eng="dve"):
    kb, nc = g.kb, g.nc
    d = st_tiles["d"]
    kb.op("dve", lambda: nc.vector.tensor_tensor(out=d[0:m, :], in0=pt[0:m, 0:TBp], in1=pt[0:m, 1:TBp + 1], op=ALU.subtract),
          R=[st_tiles["ptb"]], W=[d.b])
    kb.op("dve", lambda: nc.vector.scalar_tensor_tensor(out=out_ap, in0=d[0:m, :], scalar=mu_ap, in1=pt[0:m, 1:TBp + 1],
                                                        op0=ALU.mult, op1=ALU.add),
          R=[d.b, st_tiles["ptb"], st_tiles["mub"]], W=[st_tiles["outb"]])


def stage_rwkv_prep(g, L):
    kb, nc, T, S, NB = g.kb, g.nc, g.T, g.S, g.NB
    TBp = min(512, S)
    withv = L.withv
    nv = 8 if withv else 7
    iw0, ia0 = 0, 1
    iv0 = 2 if withv else None
    ikk, ika, irk = (3, 4, 5) if withv else (2, 3, 4)
    cw, ca = 4608, 4704
    cv = 4800 if withv else None
    cg = 4864 if withv else 4800
    NEG_E = -float(np.exp(-0.5))
    with ExitStack() as st:
        mu = Tile(kb, st, "mu", [128, 41], F32)
        vec = Tile(kb, st, "vec", [128, nv * 12], F32)
        kb.dma("sp", mu[:], L.mu[:, :], W=[mu.b], sbuf=mu.b)
        kb.dma("sp", vec[:], L.vecs[:, :], W=[vec.b], sbuf=vec.b)
        lw = Tile(kb, st, "lw", [96, SEQW], BF16)
        la = Tile(kb, st, "la", [96, SEQW], BF16)
        lg = Tile(kb, st, "lg", [128, 2, SEQW], BF16)
        kb.dma("pool", lw[:], L.lora_w[:, :], W=[lw.b], sbuf=lw.b)
        kb.dma("pool", la[:], L.lora_a[:, :], W=[la.b], sbuf=la.b)
        kb.dma("pool", lg[:], L.lora_g[:, :].rearrange("(kc p) n -> p kc n", p=128), W=[lg.b], sbuf=lg.b)
        if withv:
            lv = Tile(kb, st, "lv", [64, SEQW], BF16)
            kb.dma("pool", lv[:], L.lora_v[:, :], W=[lv.b], sbuf=lv.b)
        d = Tile(kb, st, "d", [128, TBp], F32)
        sp_ = Tile(kb, st, "sp", [128, TBp + 1], F32)
        sm = Tile(kb, st, "sm", [128, TBp], F32)
        tw = Tile(kb, st, "tw", [96, TBp], BF16)
        ta = Tile(kb, st, "ta", [96, TBp], BF16)
        tv = Tile(kb, st, "tv", [64, TBp], BF16)
        tg = Tile(kb, st, "tg", [128, 2, TBp], BF16)
        NSET = 2
        def mk(nm, dt=F32, shape=None):
            return [Tile(kb, st, nm, shape or [128, TBp], dt) for _ in range(NSET)]
        rkv = mk("rkv", shape=[128, 3, TBp + 1])
        rm, km, vm = mk("rm"), mk("km"), mk("vm")
        dec, aa, gg, kk, sq, nkk, bq, kp, rk, bon, vf = (mk("dec"), mk("aa"), mk("gg"), mk("kk"), mk("sq"), mk("nkk"),
                                                        mk("bq"), mk("kp"), mk("rk"), mk("bon"), mk("vf"))
        vt = mk("vt", shape=[128, TBp // 128, 128])
        si = 0

        def load_shift(tile_ap_fn, tb_, rows0, m, t0, bstart, nseg=None):
            pass

        for b in range(NB):
            for tl0 in range(0, S, TBp):
                t0 = b * S + tl0
                first = (tl0 == 0)

                def ld(tile, pidx, r0, m, sub=None):
                    dst = (lambda a, bb: tile[0:m, a:bb]) if sub is None else (lambda a, bb: tile[0:m, sub, a:bb])
                    if first:
                        kb.op("pool", lambda: nc.gpsimd.memset(dst(0, 1), 0.0), W=[tile.b])
                        kb.dma("sp", dst(1, TBp + 1), g.Pd[r0:r0 + m, t0:t0 + TBp], W=[tile.b], sbuf=tile.b, group=False)
                    else:
                        kb.dma("sp", dst(0, TBp + 1), g.Pd[r0:r0 + m, t0 - 1:t0 + TBp], W=[tile.b], sbuf=tile.b)

                def mixs(m, mucol, out_ap, outb):
                    _mix_tile(g, {"d": d, "ptb": sp_.b, "mub": mu.b, "outb": outb}, sp_, m, mu[0:m, mucol:mucol + 1], TBp, out_ap)

                ld(sp_, 0, cw, 96)
                mixs(96, 36, sm[0:96, :], sm.b)
                kb.op("act", lambda: nc.scalar.activation(out=tw[:], in_=sm[0:96, :], func=AF.Tanh), R=[sm.b], W=[tw.b])
                ld(sp_, 0, ca, 96)
                mixs(96, 37, sm[0:96, :], sm.b)
                kb.op("act", lambda: nc.scalar.copy(out=ta[:], in_=sm[0:96, :]), R=[sm.b], W=[ta.b])
                if withv:
                    ld(sp_, 0, cv, 64)
                    mixs(64, 38, sm[0:64, :], sm.b)
                    kb.op("act", lambda: nc.scalar.copy(out=tv[:], in_=sm[0:64, :]), R=[sm.b], W=[tv.b])
                for kc in range(2):
                    ld(sp_, 0, cg + kc * 128, 128)
                    mixs(128, 39 + kc, sm[:, :], sm.b)
                    kb.op("act", lambda kc=kc: nc.scalar.activation(out=tg[:, kc, :], in_=sm[:, :], func=AF.Sigmoid), R=[sm.b], W=[tg.b])

                for c in range(12):
                    i = si
                    si = (si + 1) % NSET
                    R3 = rkv[i]
                    for j in range(3):
                        ld(R3, 0, j * SEQW + c * 128, 128, sub=j)
                    for j, dstt in enumerate((rm[i], km[i], vm[i])):
                        kb.op("dve", lambda j=j: nc.vector.tensor_tensor(out=d[:, :], in0=R3[:, j, 0:TBp], in1=R3[:, j, 1:TBp + 1], op=ALU.subtract),
                              R=[R3.b], W=[d.b])
                        kb.op("dve", lambda j=j, dstt=dstt: nc.vector.scalar_tensor_tensor(
                            out=dstt[:], in0=d[:, :], scalar=mu[:, j * 12 + c:j * 12 + c + 1], in1=R3[:, j, 1:TBp + 1],
                            op0=ALU.mult, op1=ALU.add), R=[d.b, R3.b, mu.b], W=[dstt.b])
                    cs = slice(c * 128, (c + 1) * 128)

                    def vcol(iv):
                        return vec[:, iv * 12 + c:iv * 12 + c + 1]
                    ps, pb = psum_bank(g)
                    kb.op("pe", lambda: nc.tensor.matmul(ps[:, 0:TBp], lw[:, cs], tw[:], start=True, stop=True), R=[lw.b, tw.b], W=[pb])
                    kb.op("act", lambda: nc.scalar.activation(out=dec[i][:], in_=ps[:, 0:TBp], func=AF.Sigmoid, bias=vcol(iw0)),
                          R=[pb, vec.b], W=[dec[i].b])
                    kb.op("act", lambda: nc.scalar.activation(out=dec[i][:], in_=dec[i][:], func=AF.Exp, scale=NEG_E),
                          R=[dec[i].b], W=[dec[i].b])
                    kb.dma("sp", g.Q[1][b, cs, tl0:tl0 + TBp], dec[i][:], R=[dec[i].b], sbuf=dec[i].b)
                    ps, pb = psum_bank(g)
                    kb.op("pe", lambda: nc.tensor.matmul(ps[:, 0:TBp], la[:, cs], ta[:], start=True, stop=True), R=[la.b, ta.b], W=[pb])
                    kb.op("act", lambda: nc.scalar.activation(out=aa[i][:], in_=ps[:, 0:TBp], func=AF.Sigmoid, bias=vcol(ia0)),
                          R=[pb, vec.b], W=[aa[i].b])
                    ps, pb = psum_bank(g)
                    for kc in range(2):
                        kb.op("pe", lambda kc=kc: nc.tensor.matmul(ps[:, 0:TBp], lg[:, kc, cs], tg[:, kc, :], start=(kc == 0), stop=(kc == 1)),
                              R=[lg.b, tg.b], W=[pb])
                    kb.op("act", lambda: nc.scalar.copy(out=gg[i][:], in_=ps[:, 0:TBp]), R=[pb], W=[gg[i].b])
                    kb.dma("sp", g.Gd[cs, t0:t0 + TBp], gg[i][:], R=[gg[i].b], sbuf=gg[i].b)
                    if withv:
                        ps, pb = psum_bank(g)
                        kb.op("pe", lambda: nc.tensor.matmul(ps[:, 0:TBp], lv[:, cs], tv[:], start=True, stop=True), R=[lv.b, tv.b], W=[pb])
                        kb.op("act", lambda: nc.scalar.activation(out=sq[i][:], in_=ps[:, 0:TBp], func=AF.Sigmoid, bias=vcol(iv0)),
                              R=[pb, vec.b], W=[sq[i].b])
                        kb.dma("sp", vf[i][:], g.Vfirst[cs, t0:t0 + TBp], W=[vf[i].b], sbuf=vf[i].b)
                        kb.op("dve", lambda: nc.vector.tensor_tensor(out=vf[i][:], in0=vf[i][:], in1=vm[i][:], op=ALU.subtract),
                              R=[vf[i].b, vm[i].b], W=[vf[i].b])
                        kb.op("dve", lambda: nc.vector.tensor_tensor(out=vf[i][:], in0=vf[i][:], in1=sq[i][:], op=ALU.mult),
                              R=[vf[i].b, sq[i].b], W=[vf[i].b])
                        kb.op("dve", lambda: nc.vector.tensor_tensor(out=vm[i][:], in0=vm[i][:], in1=vf[i][:], op=ALU.add),
                              R=[vf[i].b, vm[i].b], W=[vm[i].b])
                    else:
                        kb.dma("sp", g.Vfirst[cs, t0:t0 + TBp], vm[i][:], R=[vm[i].b], sbuf=vm[i].b)
                    kb.op("dve", lambda: nc.vector.tensor_scalar(out=kk[i][:], in0=km[i][:], scalar1=vcol(ikk), scalar2=None, op0=ALU.mult),
                          R=[km[i].b, vec.b], W=[kk[i].b])
                    kb.op("act", lambda: nc.scalar.activation(out=sq[i][:], in_=kk[i][:], func=AF.Square), R=[kk[i].b], W=[sq[i].b])
                    ps, pb = psum_bank(g)
                    kb.op("pe", lambda: nc.tensor.matmul(ps[:, 0:TBp], g.blk[:], sq[i][:], start=True, stop=True), R=[g.blk.b, sq[i].b], W=[pb])
                    kb.op("act", lambda: nc.scalar.sqrt(out=sq[i][:], in_=ps[:, 0:TBp]), R=[pb], W=[sq[i].b])
                    kb.op("dve", lambda: nc.vector.tensor_scalar(out=sq[i][:], in0=sq[i][:], scalar1=1e-12, scalar2=None, op0=ALU.max),
                          R=[sq[i].b], W=[sq[i].b])
                    kb.op("dve", lambda: nc.vector.reciprocal(out=sq[i][:], in_=sq[i][:]), R=[sq[i].b], W=[sq[i].b])
                    kb.op("dve", lambda: nc.vector.scalar_tensor_tensor(out=nkk[i][:], in0=kk[i][:], scalar=-1.0, in1=sq[i][:],
                                                                        op0=ALU.mult, op1=ALU.mult), R=[kk[i].b, sq[i].b], W=[nkk[i].b])
                    kb.dma("sp", g.Q[3][b, cs, tl0:tl0 + TBp], nkk[i][:], R=[nkk[i].b], sbuf=nkk[i].b)
                    kb.op("dve", lambda: nc.vector.scalar_tensor_tensor(out=bq[i][:], in0=nkk[i][:], scalar=-1.0, in1=aa[i][:],
                                                                        op0=ALU.mult, op1=ALU.mult), R=[nkk[i].b, aa[i].b], W=[bq[i].b])
                    kb.dma("sp", g.Q[4][b, cs, tl0:tl0 + TBp], bq[i][:], R=[bq[i].b], sbuf=bq[i].b)
                    kb.op("dve", lambda: nc.vector.tensor_scalar(out=kp[i][:], in0=aa[i][:], scalar1=-1.0, scalar2=vcol(ika),
                                                                 op0=ALU.add, op1=ALU.mult), R=[aa[i].b, vec.b], W=[kp[i].b])
                    kb.op("dve", lambda: nc.vector.scalar_tensor_tensor(out=kp[i][:], in0=kp[i][:], scalar=1.0, in1=km[i][:],
                                                                        op0=ALU.add, op1=ALU.mult), R=[kp[i].b, km[i].b], W=[kp[i].b])
                    kb.dma("sp", g.Q[2][b, cs, tl0:tl0 + TBp], kp[i][:], R=[kp[i].b], sbuf=kp[i].b)
                    kb.dma("sp", g.Q[0][b, cs, tl0:tl0 + TBp], rm[i][:], R=[rm[i].b], sbuf=rm[i].b)
                    kb.op("dve", lambda: nc.vector.scalar_tensor_tensor(out=rk[i][:], in0=rm[i][:], scalar=vcol(irk), in1=kp[i][:],
                                                                        op0=ALU.mult, op1=ALU.mult), R=[rm[i].b, kp[i].b, vec.b], W=[rk[i].b])
                    ps, pb = psum_bank(g)
                    kb.op("pe", lambda: nc.tensor.matmul(ps[:, 0:TBp], g.blk[:], rk[i][:], start=True, stop=True), R=[g.blk.b, rk[i].b], W=[pb])
                    kb.op("dve", lambda: nc.vector.tensor_tensor(out=bon[i][:], in0=vm[i][:], in1=ps[:, 0:TBp], op=ALU.mult),
                          R=[pb, vm[i].b], W=[bon[i].b])
                    kb.dma("sp", g.Bonus[cs, t0:t0 + TBp], bon[i][:], R=[bon[i].b], sbuf=bon[i].b)
                    for q in range(TBp // 128):
                        ps, pb = psum_bank(g)
                        kb.op("pe", lambda q=q: nc.tensor.transpose(ps[:, 0:128], vm[i][:, q * 128:(q + 1) * 128], g.ident[:]),
                              R=[vm[i].b, g.ident.b], W=[pb])
                        kb.op("act", lambda q=q: nc.scalar.copy(out=vt[i][:, q, :], in_=ps[:, 0:128]), R=[pb], W=[vt[i].b])
                    kb.dma("sp", g.Vtok[b, tl0:tl0 + TBp, cs].rearrange("(q p) c -> p q c", p=128), vt[i][:], R=[vt[i].b], sbuf=vt[i].b)
        kb.barrier()


def stage_rwkv_scan(g, L):
    kb, nc, T, S, NB = g.kb, g.nc, g.T, g.S, g.NB
    TS = 32
    TSV = 4
    with ExitStack() as st:
        ysel = Tile(kb, st, "ysel", [128, TS, 64], F32)
        kb.dma("sp", ysel[:], g.din["c_ysel"][:, :].rearrange("p (t m) -> p t m", m=64), W=[ysel.b], sbuf=ysel.b)
        qin = [[Tile(kb, st, "qin%d" % k, [128, NH, TS], F32) for k in range(5)] for _ in range(2)]
        vbc = [Tile(kb, st, "vbc", [128, TSV, SEQW], F32) for _ in range(2)]
        H = Tile(kb, st, "H", [128, SEQW], F32)
        A = Tile(kb, st, "A", [128, SEQW], F32)
        T1 = Tile(kb, st, "T1", [128, SEQW], F32)
        T2 = Tile(kb, st, "T2", [128, SEQW], F32)
        T3 = Tile(kb, st, "T3", [128, SEQW], F32)
        T4 = Tile(kb, st, "T4", [128, SEQW], F32)
        yt = [Tile(kb, st, "yt", [64, SEQW], F32) for _ in range(2)]
        kb.op("dve", lambda: nc.vector.memset(H[:], 0.0), W=[H.b])
        yb = [g.pb[0], g.pb[1], g.pb[2]]
        ub = [[g.pb[3], g.pb[4], g.pb[5]]]
        yacc = g.ps[0:64, 0:3, :]
        upsl = [g.ps[:, 3:6, :]]

        def v3(ap):
            return ap.rearrange("p (h i) -> p h i", i=64)

        def bc(tile, tl):
            return tile[:, :, tl:tl + 1].to_broadcast([128, NH, 64])

        T3s = [T3, Tile(kb, st, "T3b", [128, SEQW], F32)]
        vtiles = {}

        def load_v(tglob):
            vb_ = vbc[(tglob // TSV) % 2]
            for b in range(NB):
                src = g.Vtok[b:b + 1, tglob:tglob + TSV, :].rearrange("o t c -> o (t c)").partition_broadcast(64)
                kb.dma("sp", vb_[b * 64:(b + 1) * 64, :, :].rearrange("p t c -> p (t c)"), src, W=[vb_.b], sbuf=vb_.b)
            vtiles[tglob // TSV] = vb_

        def emit_T3(tglob, qk, tl):
            vb_ = vtiles[tglob // TSV]
            t3 = T3s[tglob % 2]
            kb.op("pool", lambda: nc.gpsimd.tensor_tensor(out=v3(t3[:]), in0=v3(vb_[:, tglob % TSV, :]), in1=bc(qk, tl), op=ALU.mult),
                  R=[vb_.b, qk.b], W=[t3.b])

        def emit_y(pv):
            qr_p, tl_p, blk_p, y_p = pv
            kb.op("dve", lambda: nc.vector.tensor_tensor(out=v3(T4[:]), in0=v3(H[:]), in1=bc(qr_p, tl_p), op=ALU.mult),
                  R=[H.b, qr_p.b], W=[T4.b])
            yl = ysel[:, tl_p, :]
            for j in range(3):
                kb.op("pe", lambda j=j: nc.tensor.matmul(yacc[:, j, :], yl, T4[:, j * 512:(j + 1) * 512], start=(tl_p == 0), stop=(tl_p == TS - 1)),
                      R=[ysel.b, T4.b], W=[yb[j]])
            if tl_p == TS - 1:
                kb.op("act", lambda: nc.scalar.copy(out=y_p[:], in_=yacc.rearrange("p a n -> p (a n)")), R=yb, W=[y_p.b])
                for b in range(NB):
                    kb.dma("sp", g.Ytok[b, blk_p:blk_p + TS, :], y_p[b * TS:(b + 1) * TS, :], R=[y_p.b], sbuf=y_p.b)

        bi = 0
        prev = None
        ups = upsl[0]
        ubb = ub[0]
        for blk0 in range(0, S, TS):
            qs = qin[bi]
            y_ = yt[bi]
            bi ^= 1
            for b in range(NB):
                for k in range(5):
                    kb.dma("sp", qs[k][b * 64:(b + 1) * 64, :, :],
                           g.Q[k][b, :, blk0:blk0 + TS].rearrange("(h j) t -> j h t", j=64), W=[qs[k].b], sbuf=qs[k].b)
            qr, qw, qk, qn, qb = qs
            for tl in range(TS):
                tg = blk0 + tl
                if tg % TSV == 0:
                    load_v(tg)
                kb.op("pool", lambda: nc.gpsimd.tensor_tensor(out=v3(A[:]), in0=v3(H[:]), in1=bc(qw, tl), op=ALU.mult),
                      R=[H.b, qw.b], W=[A.b])
                if tl == 0:
                    emit_T3(tg, qk, tl)
                kb.op("dve", lambda: nc.vector.tensor_tensor(out=v3(T1[:]), in0=v3(H[:]), in1=bc(qn, tl), op=ALU.mult),
                      R=[H.b, qn.b], W=[T1.b])
                for j in range(3):
                    kb.op("pe", lambda j=j: nc.tensor.matmul(ups[:, j, :], g.blk[:], T1[:, j * 512:(j + 1) * 512], start=True, stop=True),
                          R=[g.blk.b, T1.b], W=[ubb[j]])
                if prev is not None:
                    emit_y(prev)
                if tl + 1 < TS:
                    if (tg + 1) % TSV == 0:
                        load_v(tg + 1)
                    emit_T3(tg + 1, qk, tl + 1)
                t3 = T3s[tg % 2]
                kb.op("dve", lambda: nc.vector.tensor_tensor(out=v3(T2[:]), in0=ups.rearrange("p a (h i) -> p (a h) i", i=64),
                                                             in1=bc(qb, tl), op=ALU.mult), R=ubb + [qb.b], W=[T2.b])
                kb.op("dve", lambda: nc.vector.tensor_tensor(out=A[:], in0=A[:], in1=t3[:], op=ALU.add), R=[A.b, t3.b], W=[A.b])
                kb.op("dve", lambda: nc.vector.tensor_tensor(out=H[:], in0=A[:], in1=T2[:], op=ALU.add), R=[A.b, T2.b], W=[H.b])
                prev = (qr, tl, blk0, y_)
        emit_y(prev)
        kb.barrier()


def stage_rwkv_post(g, L):
    kb, nc, T, S, NB = g.kb, g.nc, g.T, g.S, g.NB
    nv = 8 if L.withv else 7
    ign, igb = nv - 2, nv - 1
    with ExitStack() as st:
        vec = Tile(kb, st, "vec", [128, nv * 12], F32)
        kb.dma("sp", vec[:], L.vecs[:, :], W=[vec.b], sbuf=vec.b)
        ytk = [Tile(kb, st, "ytk", [128, SEQW], F32) for _ in range(2)]
        sqt = Tile(kb, st, "sqt", [128, SEQW], F32)
        s1 = Tile(kb, st, "s1", [128, NH], F32)
        s2 = Tile(kb, st, "s2", [128, NH], F32)
        bon = [Tile(kb, st, "bon", [128, 12, 128], F32) for _ in range(2)]
        gt = [Tile(kb, st, "gt", [128, 12, 128], F32) for _ in range(2)]
        fm = Tile(kb, st, "fm", [128, 12, 128], F32)
        ob = [Tile(kb, st, "ob", [128, 12, 128], BF16) for _ in range(2)]

        def v3(ap):
            return ap.rearrange("p (h i) -> p h i", i=64)
        i = 0
        for t0 in range(0, T, 128):
            b = t0 // S
            tl0 = t0 - b * S
            y = ytk[i]
            bo = bon[i]
            gq = gt[i]
            o = ob[i]
            i ^= 1
            kb.dma("sp", y[:], g.Ytok[b, tl0:tl0 + 128, :], W=[y.b], sbuf=y.b)
            kb.dma("sp", bo[:], g.Bonus[:, t0:t0 + 128].rearrange("(c p) t -> p c t", p=128), W=[bo.b], sbuf=bo.b)
            kb.dma("sp", gq[:], g.Gd[:, t0:t0 + 128].rearrange("(c p) t -> p c t", p=128), W=[gq.b], sbuf=gq.b)
            kb.op("dve", lambda: nc.vector.tensor_reduce(out=s1[:], in_=v3(y[:]), axis=AX.X, op=ALU.add), R=[y.b], W=[s1.b])
            kb.op("dve", lambda: nc.vector.tensor_scalar(out=s1[:], in0=s1[:], scalar1=1.0 / 64, scalar2=None, op0=ALU.mult), R=[s1.b], W=[s1.b])
            kb.op("dve", lambda: nc.vector.tensor_tensor(out=v3(y[:]), in0=v3(y[:]), in1=s1[:].unsqueeze(2).to_broadcast([128, NH, 64]),
                                                         op=ALU.subtract), R=[y.b, s1.b], W=[y.b])
            kb.op("act", lambda: nc.scalar.activation(out=sqt[:], in_=y[:], func=AF.Square), R=[y.b], W=[sqt.b])
            kb.op("dve", lambda: nc.vector.tensor_reduce(out=s2[:], in_=v3(sqt[:]), axis=AX.X, op=ALU.add), R=[sqt.b], W=[s2.b])
            kb.op("dve", lambda: nc.vector.tensor_scalar(out=s2[:], in0=s2[:], scalar1=1.0 / 64, scalar2=GN_EPS_RWKV, op0=ALU.mult, op1=ALU.add),
                  R=[s2.b], W=[s2.b])
            kb.op("act", lambda: nc.scalar.sqrt(out=s2[:], in_=s2[:]), R=[s2.b], W=[s2.b])
            kb.op("dve", lambda: nc.vector.reciprocal(out=s2[:], in_=s2[:]), R=[s2.b], W=[s2.b])
            kb.op("dve", lambda: nc.vector.tensor_tensor(out=v3(y[:]), in0=v3(y[:]), in1=s2[:].unsqueeze(2).to_broadcast([128, NH, 64]),
                                                         op=ALU.mult), R=[y.b, s2.b], W=[y.b])
            for c in range(12):
                ps, pb = psum_bank(g)
                kb.op("pe", lambda c=c: nc.tensor.transpose(ps[:, 0:128], y[:, c * 128:(c + 1) * 128], g.ident[:]), R=[y.b, g.ident.b], W=[pb])
                kb.op("act", lambda c=c: nc.scalar.activation(out=fm[:, c, :], in_=ps[:, 0:128], func=AF.Identity,
                                                              scale=vec[:, ign * 12 + c:ign * 12 + c + 1],
                                                              bias=vec[:, igb * 12 + c:igb * 12 + c + 1]), R=[pb, vec.b], W=[fm.b])
            kb.op("dve", lambda: nc.vector.tensor_tensor(out=fm[:], in0=fm[:], in1=bo[:], op=ALU.add), R=[fm.b, bo.b], W=[fm.b])
            kb.op("dve", lambda: nc.vector.tensor_tensor(out=o[:], in0=fm[:], in1=gq[:], op=ALU.mult), R=[fm.b, gq.b], W=[o.b])
            kb.dma("sp", g.Ymix[0:SEQW, t0:t0 + 128].rearrange("(c p) t -> p c t", p=128), o[:], R=[o.b], sbuf=o.b)
        kb.barrier()


def stage_retention(g, L):
    kb, nc, T, S, NB = g.kb, g.nc, g.T, g.S, g.NB
    PI = float(np.pi)
    scale = float(128 ** -0.5)
    with ExitStack() as st:
        gn = Tile(kb, st, "gn", [128, 24], F32)
        rot = Tile(kb, st, "rot", [128, 4], F32)
        retT = Tile(kb, st, "retT", [128, RET_H, 128], F32)
        retq = Tile(kb, st, "retq", [128, RET_H, 128], F32)
        retk = Tile(kb, st, "retk", [128, RET_H], F32)
        retc = Tile(kb, st, "retc", [128, RET_H], F32)
        kb.dma("sp", gn[:], L.gn[:, :], W=[gn.b], sbuf=gn.b)
        kb.dma("sp", rot[:], g.din["c_rot"][:, :], W=[rot.b], sbuf=rot.b)
        kb.dma("sp", retT[:], g.din["c_retT"][:, :].rearrange("p (h n) -> p h n", n=128), W=[retT.b], sbuf=retT.b)
        kb.dma("sp", retq[:], g.din["c_retq"][:, :].rearrange("p (h n) -> p h n", n=128), W=[retq.b], sbuf=retq.b)
        kb.dma("sp", retk[:], g.din["c_retk"][:, :], W=[retk.b], sbuf=retk.b)
        kb.dma("sp", retc[:], g.din["c_retc"][:, :], W=[retc.b], sbuf=retc.b)
        QK = [Tile(kb, st, "QK", [128, 12, 128], F32) for _ in range(2)]
        VV = [Tile(kb, st, "VV", [128, 12, 128], F32) for _ in range(2)]
        GG = [Tile(kb, st, "GG", [128, 12, 128], F32) for _ in range(2)]
        OB = [Tile(kb, st, "OB", [128, 12, 128], BF16) for _ in range(2)]
        posi = Tile(kb, st, "posi", [128, 128], I32)
        ang = Tile(kb, st, "ang", [128, 128], F32)
        cs = Tile(kb, st, "cs", [128, 128], F32)
        sn = Tile(kb, st, "sn", [128, 128], F32)
        csk = Tile(kb, st, "csk", [128, 128], F32)
        snk = Tile(kb, st, "snk", [128, 128], F32)
        t1 = Tile(kb, st, "t1", [128, 128], F32)
        t2 = Tile(kb, st, "t2", [128, 128], F32)
        qr = Tile(kb, st, "qr", [128, 128], F32)
        kr = Tile(kb, st, "kr", [128, 128], F32)
        qb = Tile(kb, st, "qb", [128, 128], BF16)
        kbb = Tile(kb, st, "kbb", [128, 128], BF16)
        qd = Tile(kb, st, "qd", [128, 128], BF16)
        sTd = Tile(kb, st, "sTd", [128, 128], BF16)
        ktd = Tile(kb, st, "ktd", [128, 128], BF16)
        vtk = Tile(kb, st, "vtk", [128, 256], BF16)
        oo = Tile(kb, st, "oo", [128, 2, 128], F32)
        sq = Tile(kb, st, "sq", [128, 2, 128], F32)
        mean = Tile(kb, st, "mean", [128, 128], F32)
        rstd = Tile(kb, st, "rstd", [128, 128], F32)
        sg = Tile(kb, st, "sg", [128, 128], F32)
        Rs = [Tile(kb, st, "R", [128, 256], F32) for _ in range(RET_H)]
        Rb = [Tile(kb, st, "Rb", [128, 256], BF16) for _ in range(RET_H)]
        i = 0
        for t0 in range(0, T, 128):
            b = t0 // S
            if t0 % S == 0:
                for h in range(RET_H):
                    kb.op("dve", lambda h=h: nc.vector.memset(Rs[h][:], 0.0), W=[Rs[h].b])
                    kb.op("dve", lambda h=h: nc.vector.memset(Rb[h][:], 0.0), W=[Rb[h].b])
            X, V, Gt, O = QK[i], VV[i], GG[i], OB[i]
            i ^= 1
            kb.dma("sp", X[:], g.Pd[0:1536, t0:t0 + 128].rearrange("(c p) t -> p c t", p=128), W=[X.b], sbuf=X.b)
            kb.dma("sp", V[:], g.Pd[1536:3072, t0:t0 + 128].rearrange("(c p) t -> p c t", p=128), W=[V.b], sbuf=V.b)
            kb.dma("sp", Gt[:], g.Pd[3072:4608, t0:t0 + 128].rearrange("(c p) t -> p c t", p=128), W=[Gt.b], sbuf=Gt.b)
            kb.dma("sp", posi[:], g.din["pos"][0:1, t0:t0 + 128].partition_broadcast(128), W=[posi.b], sbuf=posi.b)
            kb.op("dve", lambda: nc.vector.tensor_copy(out=ang[:], in_=posi[:]), R=[posi.b], W=[ang.b])
            kb.op("dve", lambda: nc.vector.tensor_scalar(out=ang[:], in0=ang[:], scalar1=rot[:, 0:1], scalar2=None, op0=ALU.mult),
                  R=[ang.b, rot.b], W=[ang.b])
            C1 = 6.28125
            C2 = float(2 * np.pi - 6.28125)
            for dst, off in ((sn, 0.0), (cs, 0.25)):
                kb.op("dve", lambda off=off: nc.vector.tensor_scalar(out=t1[:], in0=ang[:], scalar1=float(1.0 / (2 * np.pi)), scalar2=off,
                                                                     op0=ALU.mult, op1=ALU.add), R=[ang.b], W=[t1.b])
                kb.op("dve", lambda: nc.vector.tensor_copy(out=posi[:], in_=t1[:]), R=[t1.b], W=[posi.b])
                kb.op("dve", lambda: nc.vector.tensor_copy(out=t2[:], in_=posi[:]), R=[posi.b], W=[t2.b])
                kb.op("dve", lambda: nc.vector.tensor_tensor(out=t1[:], in0=t1[:], in1=t2[:], op=ALU.subtract), R=[t1.b, t2.b], W=[t1.b])
                kb.op("dve", lambda: nc.vector.tensor_scalar(out=t1[:], in0=t1[:], scalar1=0.5, scalar2=None, op0=ALU.is_ge), R=[t1.b], W=[t1.b])
                kb.op("dve", lambda: nc.vector.tensor_tensor(out=t2[:], in0=t2[:], in1=t1[:], op=ALU.add), R=[t1.b, t2.b], W=[t2.b])
                kb.op("dve", lambda dst=dst: nc.vector.scalar_tensor_tensor(out=dst[:], in0=t2[:], scalar=-C1, in1=ang[:],
                                                                            op0=ALU.mult, op1=ALU.add), R=[t2.b, ang.b], W=[dst.b])
                kb.op("dve", lambda dst=dst: nc.vector.scalar_tensor_tensor(out=dst[:], in0=t2[:], scalar=-C2, in1=dst[:],
                                                                            op0=ALU.mult, op1=ALU.add), R=[t2.b, dst.b], W=[dst.b])
                if off:
                    kb.op("dve", lambda dst=dst: nc.vector.tensor_scalar(out=dst[:], in0=dst[:], scalar1=float(np.pi / 2), scalar2=None,
                                                                         op0=ALU.add), R=[dst.b], W=[dst.b])
                kb.op("dve", lambda dst=dst: nc.vector.tensor_scalar(out=dst[:], in0=dst[:], scalar1=-PI, scalar2=PI, op0=ALU.max, op1=ALU.min),
                      R=[dst.b], W=[dst.b])
                kb.op("act", lambda dst=dst: nc.scalar.activation(out=dst[:], in_=dst[:], func=AF.Sin), R=[dst.b], W=[dst.b])
            kb.op("dve", lambda: nc.vector.tensor_scalar(out=sn[:], in0=sn[:], scalar1=rot[:, 1:2], scalar2=None, op0=ALU.mult),
                  R=[sn.b, rot.b], W=[sn.b])
            kb.op("dve", lambda: nc.vector.tensor_scalar(out=csk[:], in0=cs[:], scalar1=scale, scalar2=None, op0=ALU.mult), R=[cs.b], W=[csk.b])
            kb.op("dve", lambda: nc.vector.tensor_scalar(out=snk[:], in0=sn[:], scalar1=scale, scalar2=None, op0=ALU.mult), R=[sn.b], W=[snk.b])
            for h in range(RET_H):
                for (ch, c_t, s_t, dstt) in ((h, cs, sn, qr), (6 + h, csk, snk, kr)):
                    ps, pb = psum_bank(g)
                    kb.op("pe", lambda ch=ch: nc.tensor.matmul(ps[:, 0:128], g.swap[:], X[:, ch, :], start=True, stop=True),
                          R=[g.swap.b, X.b], W=[pb])
                    kb.op("dve", lambda ch=ch, c_t=c_t: nc.vector.tensor_tensor(out=t1[:], in0=X[:, ch, :], in1=c_t[:], op=ALU.mult),
                          R=[X.b, c_t.b], W=[t1.b])
                    kb.op("dve", lambda s_t=s_t: nc.vector.tensor_tensor(out=t2[:], in0=s_t[:], in1=ps[:, 0:128], op=ALU.mult),
                          R=[pb, s_t.b], W=[t2.b])
                    kb.op("dve", lambda dstt=dstt: nc.vector.tensor_tensor(out=dstt[:], in0=t1[:], in1=t2[:], op=ALU.add),
                          R=[t1.b, t2.b], W=[dstt.b])
                kb.op("act", lambda: nc.scalar.copy(out=qb[:], in_=qr[:]), R=[qr.b], W=[qb.b])
                kb.op("act", lambda: nc.scalar.copy(out=kbb[:], in_=kr[:]), R=[kr.b], W=[kbb.b])
                kb.op("dve", lambda: nc.vector.tensor_tensor(out=qd[:], in0=qr[:], in1=retq[:, h, :], op=ALU.mult), R=[qr.b, retq.b], W=[qd.b])
                ps, pb = psum_bank(g)
                kb.op("pe", lambda: nc.tensor.matmul(ps[:, 0:128], kbb[:], qb[:], start=True, stop=True), R=[kbb.b, qb.b], W=[pb])
                kb.op("dve", lambda: nc.vector.tensor_tensor(out=sTd[:], in0=retT[:, h, :], in1=ps[:, 0:128], op=ALU.mult),
                      R=[pb, retT.b], W=[sTd.b])
                ps, pb = psum_bank(g)
                kb.op("pe", lambda: nc.tensor.transpose(ps[:, 0:128], kr[:], g.ident[:]), R=[kr.b, g.ident.b], W=[pb])
                kb.op("act", lambda: nc.scalar.activation(out=ktd[:], in_=ps[:, 0:128], func=AF.Copy, scale=retk[:, h:h + 1]),
                      R=[pb, retk.b], W=[ktd.b])
                for ec in range(2):
                    ps, pb = psum_bank(g)
                    kb.op("pe", lambda ec=ec: nc.tensor.transpose(ps[:, 0:128], V[:, h * 2 + ec, :], g.ident[:]), R=[V.b, g.ident.b], W=[pb])
                    kb.op("act", lambda ec=ec: nc.scalar.copy(out=vtk[:, ec * 128:(ec + 1) * 128], in_=ps[:, 0:128]), R=[pb], W=[vtk.b])
                for ec in range(2):
                    ps, pb = psum_bank(g)
                    kb.op("pe", lambda ec=ec: nc.tensor.matmul(ps[:, 0:128], vtk[:, ec * 128:(ec + 1) * 128], sTd[:], start=True, stop=False),
                          R=[vtk.b, sTd.b], W=[pb])
                    kb.op("pe", lambda ec=ec: nc.tensor.matmul(ps[:, 0:128], Rb[h][:, ec * 128:(ec + 1) * 128], qd[:], start=False, stop=True),
                          R=[Rb[h].b, qd.b], W=[pb])
                    kb.op("act", lambda ec=ec: nc.scalar.copy(out=oo[:, ec, :], in_=ps[:, 0:128]), R=[pb], W=[oo.b])
                ps, pb = psum_bank(g)
                kb.op("pe", lambda: nc.tensor.matmul(ps[:, 0:256], ktd[:], vtk[:], start=True, stop=True), R=[ktd.b, vtk.b], W=[pb])
                kb.op("dve", lambda: nc.vector.scalar_tensor_tensor(out=Rs[h][:], in0=Rs[h][:], scalar=retc[:, h:h + 1], in1=ps[:, 0:256],
                                                                    op0=ALU.mult, op1=ALU.add), R=[Rs[h].b, retc.b, pb], W=[Rs[h].b])
                kb.op("act", lambda: nc.scalar.copy(out=Rb[h][:], in_=Rs[h][:]), R=[Rs[h].b], W=[Rb[h].b])
                ps, pb = psum_bank(g)
                for ec in range(2):
                    kb.op("pe", lambda ec=ec: nc.tensor.matmul(ps[:, 0:128], g.ones[:], oo[:, ec, :], start=(ec == 0), stop=(ec == 1)),
                          R=[g.ones.b, oo.b], W=[pb])
                kb.op("act", lambda: nc.scalar.mul(out=mean[:], in_=ps[:, 0:128], mul=1.0 / 256), R=[pb], W=[mean.b])
                kb.op("dve", lambda: nc.vector.tensor_tensor(out=oo[:], in0=oo[:], in1=mean[:].unsqueeze(1).to_broadcast([128, 2, 128]),
                                                             op=ALU.subtract), R=[oo.b, mean.b], W=[oo.b])
                kb.op("act", lambda: nc.scalar.activation(out=sq[:], in_=oo[:], func=AF.Square), R=[oo.b], W=[sq.b])
                ps, pb = psum_bank(g)
                for ec in range(2):
                    kb.op("pe", lambda ec=ec: nc.tensor.matmul(ps[:, 0:128], g.ones[:], sq[:, ec, :], start=(ec == 0), stop=(ec == 1)),
                          R=[g.ones.b, sq.b], W=[pb])
                kb.op("dve", lambda: nc.vector.tensor_scalar(out=rstd[:], in0=ps[:, 0:128], scalar1=1.0 / 256, scalar2=LN_EPS,
                                                             op0=ALU.mult, op1=ALU.add), R=[pb], W=[rstd.b])
                kb.op("act", lambda: nc.scalar.sqrt(out=rstd[:], in_=rstd[:]), R=[rstd.b], W=[rstd.b])
                kb.op("dve", lambda: nc.vector.reciprocal(out=rstd[:], in_=rstd[:]), R=[rstd.b], W=[rstd.b])
                kb.op("dve", lambda: nc.vector.tensor_tensor(out=oo[:], in0=oo[:], in1=rstd[:].unsqueeze(1).to_broadcast([128, 2, 128]),
                                                             op=ALU.mult), R=[oo.b, rstd.b], W=[oo.b])
                for ec in range(2):
                    cix = h * 2 + ec
                    kb.op("act", lambda ec=ec, cix=cix: nc.scalar.activation(out=oo[:, ec, :], in_=oo[:, ec, :], func=AF.Identity,
                                                                             scale=gn[:, cix:cix + 1], bias=gn[:, 12 + cix:12 + cix + 1]),
                          R=[oo.b, gn.b], W=[oo.b])
                    kb.op("act", lambda cix=cix: nc.scalar.activation(out=sg[:], in_=Gt[:, cix, :], func=AF.Silu), R=[Gt.b], W=[sg.b])
                    kb.op("dve", lambda ec=ec, cix=cix: nc.vector.tensor_tensor(out=O[:, cix, :], in0=oo[:, ec, :], in1=sg[:], op=ALU.mult),
                          R=[oo.b, sg.b], W=[O.b])
            kb.dma("sp", g.Ymix[0:SEQW, t0:t0 + 128].rearrange("(c p) t -> p c t", p=128), O[:], R=[O.b], sbuf=O.b)
        kb.barrier()


def consts():
    c = {}
    c["c_ident"] = np.eye(128, dtype=np.float32)
    c["c_ones"] = np.ones((128, 128), np.float32)
    blk = np.zeros((128, 128), np.float32)
    blk[:64, :64] = 1
    blk[64:, 64:] = 1
    c["c_blk"] = blk
    sw = np.zeros((128, 128), np.float32)
    for i in range(64):
        sw[i, i + 64] = 1
        sw[i + 64, i] = 1
    c["c_swap"] = sw
    ys = np.zeros((128, 32, 64), np.float32)
    for k in range(128):
        for tl in range(32):
            ys[k, tl, (k // 64) * 32 + tl] = 1
    c["c_ysel"] = ys.reshape(128, 32 * 64)
    rot = np.zeros((128, 4), np.float32)
    inv = (1.0 / (np.float32(10000.0) ** (np.arange(0, 128, 2, dtype=np.float32) / np.float32(128)))).astype(np.float32)
    rot[:64, 0] = inv
    rot[64:, 0] = inv
    rot[:64, 1] = -1
    rot[64:, 1] = 1
    c["c_rot"] = rot
    Cc = 128
    lg = np.log1p(-np.exp2(-5.0 - np.arange(RET_H, dtype=np.float64)))
    idx = np.arange(Cc, dtype=np.float64)
    diff = idx[:, None] - idx[None, :]
    retT = np.zeros((128, RET_H, 128), np.float32)
    retq = np.zeros((128, RET_H, 128), np.float32)
    retk = np.zeros((128, RET_H), np.float32)
    retc = np.zeros((128, RET_H), np.float32)
    for h in range(RET_H):
        intra = np.where(diff >= 0, np.exp(np.maximum(diff, 0) * lg[h]), 0.0)
        retT[:, h, :] = intra.T
        retq[:, h, :] = np.exp((idx + 1.0) * lg[h])[None, :]
        retk[:, h] = np.exp((Cc - 1.0 - idx) * lg[h])
        retc[:, h] = np.exp(Cc * lg[h])
    c["c_retT"] = retT.reshape(128, -1)
    c["c_retq"] = retq.reshape(128, -1)
    c["c_retk"] = retk
    c["c_retc"] = retc
    return c


def prep_weights(inputs):
    return prep_weights_layers(inputs, [0, 1, 2, 3])


def prep_weights_layers(inputs, layers):
    w = {}
    w["mem_w_kv"] = np.ascontiguousarray(inputs["mem_w_kv"], dtype=np.float32)
    for li in layers:
        p = "l%d_" % li
        w[p + "w_in"] = inputs[p + "w_in"]
        w[p + "w_out"] = inputs[p + "w_out"]
        nr = np.asarray(inputs[p + "norms"], np.float32)
        w[p + "norms"] = np.concatenate([colvec(nr[i]) for i in range(4)], axis=1)
        if li % 2 == 0:
            muv = np.asarray(inputs[p + "mu"], np.float32)
            mus = np.zeros((128, 41), np.float32)
            mus[:, 0:36] = colvec(muv[0:4608])
            mus[0:96, 36] = muv[4608:4704]
            mus[0:96, 37] = muv[4704:4800]
            cgo = 4800
            if li == 2:
                mus[0:64, 38] = muv[4800:4864]
                cgo = 4864
            mus[:, 39] = muv[cgo:cgo + 128]
            mus[:, 40] = muv[cgo + 128:cgo + 256]
            w[p + "mu"] = mus
            for k in ("lora_w", "lora_a", "lora_g", "ffn_gate", "ffn_up", "ffn_down"):
                w[p + k] = inputs[p + k]
            if li == 2:
                w[p + "lora_v"] = inputs[p + "lora_v"]
            vv = np.asarray(inputs[p + "vecs"], np.float32)
            w[p + "vecs"] = np.concatenate([colvec(vv[i]) for i in range(vv.shape[0])], axis=1)
        else:
            gg = np.asarray(inputs[p + "gn"], np.float32)
            w[p + "gn"] = np.concatenate([colvec(gg[i]) for i in range(2)], axis=1)
            w[p + "router"] = inputs[p + "router"]
            w[p + "moe_gate"] = np.asarray(inputs[p + "moe_gate"]).reshape(NE * D, DFF)
            w[p + "moe_up"] = np.asarray(inputs[p + "moe_up"]).reshape(NE * D, DFF)
            w[p + "moe_down"] = np.asarray(inputs[p + "moe_down"]).reshape(NE * DFF, D)
    return w


def core_inputs(inputs, w, c, core, NB, S):
    x = np.asarray(inputs["x"])[core * NB:(core + 1) * NB, :S]
    m = {}
    m["xT"] = np.ascontiguousarray(x.reshape(NB * S, D).T)
    mem = np.asarray(inputs["mem"])[core * NB:(core + 1) * NB]
    m["memT"] = np.ascontiguousarray(mem.reshape(NB * MEMLEN, D).T)
    m["pos"] = np.ascontiguousarray(np.asarray(inputs["positions"])[core * NB:(core + 1) * NB, :S].reshape(1, NB * S)).astype(np.int32)
    m.update(w)
    m.update(c)
    return m


_CACHE = {}


def kernel(**inputs):
    NB, S, NCORE = 2, 2048, 8
    if "nc" not in _CACHE:
        _CACHE["nc"] = build(NB, S, [0, 1, 2, 3]).nc
    nc = _CACHE["nc"]
    w = prep_weights(inputs)
    c = consts()
    in_maps = [core_inputs(inputs, w, c, core, NB, S) for core in range(NCORE)]
    res = run_bass_kernel_spmd(nc, in_maps, core_ids=list(range(NCORE)))
    out = np.empty((NCORE * NB, S, D), np.float32)
    for core in range(NCORE):
        oT = res.results[core]["outT"]
        out[core * NB:(core + 1) * NB] = oT.T.reshape(NB, S, D)
    return out
```
